# Optimizing a Trainium2 kernel written in Bass

```python
import jax
import jax.numpy as jnp
from jax import lax
import numpy as np

D_MODEL = 2048
BATCH = 2
SEQ = 4096
DEPTH = 4

GRID_W = 64
CTX_LEN = 256
EPS = 1e-6
CONV_DIM = D_MODEL // 2
CONV_WIDTH = 31
SGU_DIM = D_MODEL // 2
SGU_CHUNK = 128
SGU_GROUPS = SGU_DIM // 128
NA_HEAD_DIM = 64
NA_HEADS = (D_MODEL // 2) // NA_HEAD_DIM
NA_DIM = NA_HEADS * NA_HEAD_DIM
NA_KH = 8
NA_KW = 16
N_BRANCH = 3
D_FF = 4 * D_MODEL
Q_OFF = 2 * CONV_DIM + 2 * SGU_DIM
K_OFF = Q_OFF + NA_DIM
V_OFF = K_OFF + NA_DIM
G_OFF = V_OFF + NA_DIM
IN_DIM = G_OFF + N_BRANCH * D_MODEL
NEG_INF = -1e30

kernel_name = 'hybrid_conv_sgu_natten_prefix_block'


def rms_norm(x, g):
    xf = x.astype(jnp.float32)
    y = xf * lax.rsqrt(jnp.mean(xf * xf, axis=-1, keepdims=True) + EPS)
    return (y * g.astype(jnp.float32)).astype(x.dtype)


def layer_norm(x, g, b):
    xf = x.astype(jnp.float32)
    xc = xf - jnp.mean(xf, axis=-1, keepdims=True)
    var = jnp.mean(xc * xc, axis=-1, keepdims=True)
    return (xc * lax.rsqrt(var + EPS) * g.astype(jnp.float32) + b.astype(jnp.float32)).astype(x.dtype)


def modulate(h, shift, scale):
    return h * (1 + scale[:, None, :]) + shift[:, None, :]


def split_heads(t):
    return t.reshape(t.shape[0], t.shape[1], NA_HEADS, NA_HEAD_DIM)


def conv_branch(a, conv_w, conv_b, ln_g, ln_b, w_out):
    a1, a2 = jnp.split(a, 2, axis=-1)
    u = a1 * jax.nn.sigmoid(a2)
    pad = CONV_WIDTH // 2
    u = lax.conv_general_dilated(u, conv_w[:, None, :], window_strides=(1,), padding=[(pad, pad)],
                                 dimension_numbers=('NWC', 'WIO', 'NWC'),
                                 feature_group_count=CONV_DIM) + conv_b
    u = jax.nn.silu(layer_norm(u, ln_g, ln_b))
    return u @ w_out


def sgu_branch(uv, ln_g, ln_b, w_s, b_s, w_out):
    u, v = jnp.split(jax.nn.gelu(uv), 2, axis=-1)
    v = layer_norm(v, ln_g, ln_b)
    bsz, n, _ = v.shape
    v = v.reshape(bsz, n // SGU_CHUNK, SGU_CHUNK, SGU_GROUPS, SGU_DIM // SGU_GROUPS)
    v = jnp.einsum('gpq,bnqgc->bnpgc', w_s, v) + b_s.T[None, None, :, :, None]
    return (u * v.reshape(bsz, n, SGU_DIM)) @ w_out


def neighbourhood_attention(q, k, v, kc, vc, rpb):
    bsz, s, h, dh = q.shape
    rows = s // GRID_W
    kh = min(NA_KH, rows)
    ncb = GRID_W // NA_KW
    kbw = 2 * NA_KW
    r = np.arange(rows)
    row_idx = np.clip(r - kh // 2, 0, rows - kh)[:, None] + np.arange(kh)[None]
    dr = row_idx - r[:, None]
    j = np.arange(ncb)
    col_idx = np.clip(j * NA_KW - NA_KW // 2, 0, GRID_W - kbw)[:, None] + np.arange(kbw)[None]
    qc = j[:, None] * NA_KW + np.arange(NA_KW)[None]
    win_start = np.clip(qc - NA_KW // 2, 0, GRID_W - NA_KW)
    off = col_idx[:, None, :] - win_start[:, :, None]
    col_valid = (off >= 0) & (off < NA_KW)
    dc = np.clip(col_idx[:, None, :] - qc[:, :, None] + NA_KW - 1, 0, 2 * NA_KW - 2)
    bias = rpb[:, (dr + NA_KH - 1)[:, None, None, :, None], dc[None, :, :, None, :]]
    bias = jnp.where(col_valid[None, None, :, :, None, :], bias.astype(jnp.float32), NEG_INF)

    qg = q.reshape(bsz, rows, ncb, NA_KW, h, dh) * NA_HEAD_DIM ** -0.5
    kg = k.reshape(bsz, rows, GRID_W, h, dh)
    vg = v.reshape(bsz, rows, GRID_W, h, dh)
    ri = row_idx[:, None, :, None]
    ci = col_idx[None, :, None, :]
    k_blk = kg[:, ri, ci]
    v_blk = vg[:, ri, ci]
    s_loc = jnp.einsum('brjqhd,brjiwhd->bhrjqiw', qg, k_blk).astype(jnp.float32) + bias[None]
    s_loc = s_loc.reshape(bsz, h, rows, ncb, NA_KW, kh * kbw)
    s_ctx = jnp.einsum('brjqhd,bchd->bhrjqc', qg, kc).astype(jnp.float32)
    p = jax.nn.softmax(jnp.concatenate([s_loc, s_ctx], axis=-1), axis=-1).astype(v.dtype)
    p_loc = p[..., :kh * kbw].reshape(bsz, h, rows, ncb, NA_KW, kh, kbw)
    p_ctx = p[..., kh * kbw:]
    o = (jnp.einsum('bhrjqiw,brjiwhd->brjqhd', p_loc, v_blk)
         + jnp.einsum('bhrjqc,bchd->brjqhd', p_ctx, vc))
    return o.reshape(bsz, s, NA_DIM)


def context_attention(q, k, v):
    s = jnp.einsum('bqhd,bkhd->bhqk', q * NA_HEAD_DIM ** -0.5, k).astype(jnp.float32)
    p = jax.nn.softmax(s, axis=-1).astype(v.dtype)
    o = jnp.einsum('bhqk,bkhd->bqhd', p, v)
    return o.reshape(o.shape[0], o.shape[1], NA_DIM)


def mixer_output(z, y_na, conv_w, conv_b, conv_ln_g, conv_ln_b, w_conv_out, sgu_ln_g, sgu_ln_b,
                 sgu_w, sgu_b, w_sgu_out, w_na_out, w_o, b_o):
    y_conv = conv_branch(z[..., :2 * CONV_DIM], conv_w, conv_b, conv_ln_g, conv_ln_b, w_conv_out)
    y_sgu = sgu_branch(z[..., 2 * CONV_DIM:Q_OFF], sgu_ln_g, sgu_ln_b, sgu_w, sgu_b, w_sgu_out)
    y_att = y_na @ w_na_out
    g_conv, g_sgu, g_att = jnp.split(jax.nn.sigmoid(z[..., G_OFF:]), N_BRANCH, axis=-1)
    return (g_conv * y_conv + g_sgu * y_sgu + g_att * y_att) @ w_o + b_o


def sq_relu_ffn(h, w1, b1, w2, b2):
    return jnp.square(jax.nn.relu(h @ w1 + b1)) @ w2 + b2


def setup_inputs(seed: int = 0) -> dict:
    key = jax.random.key(seed)
    ks = iter(jax.random.split(key, 40))
    f32 = jnp.float32

    def nrm(shape, scale):
        return jax.random.normal(next(ks), shape, f32) * scale

    L = DEPTH
    return {
        'x': nrm((BATCH, SEQ, D_MODEL), 1.0),
        'c': nrm((BATCH, D_MODEL), 1.0),
        'ctx': nrm((BATCH, CTX_LEN, D_MODEL), 1.0),
        'c_ctx': nrm((D_MODEL,), 1.0),
        'w_mod': nrm((L, D_MODEL, 6 * D_MODEL), 0.5 * D_MODEL ** -0.5),
        'b_mod': nrm((L, 6 * D_MODEL), 0.01),
        'norm1_g': 1.0 + nrm((L, D_MODEL), 0.05),
        'norm2_g': 1.0 + nrm((L, D_MODEL), 0.05),
        'w_in': nrm((L, D_MODEL, IN_DIM), D_MODEL ** -0.5),
        'b_in': nrm((L, IN_DIM), 0.01),
        'conv_w': nrm((L, CONV_WIDTH, CONV_DIM), CONV_WIDTH ** -0.5),
        'conv_b': nrm((L, CONV_DIM), 0.01),
        'conv_ln_g': 1.0 + nrm((L, CONV_DIM), 0.05),
        'conv_ln_b': nrm((L, CONV_DIM), 0.01),
        'w_conv_out': nrm((L, CONV_DIM, D_MODEL), CONV_DIM ** -0.5),
        'sgu_ln_g': 1.0 + nrm((L, SGU_DIM), 0.05),
        'sgu_ln_b': nrm((L, SGU_DIM), 0.01),
        'sgu_w': nrm((L, SGU_GROUPS, SGU_CHUNK, SGU_CHUNK), SGU_CHUNK ** -0.5),
        'sgu_b': 1.0 + nrm((L, SGU_GROUPS, SGU_CHUNK), 0.1),
        'w_sgu_out': nrm((L, SGU_DIM, D_MODEL), SGU_DIM ** -0.5),
        'na_rpb': nrm((L, NA_HEADS, 2 * NA_KH - 1, 2 * NA_KW - 1), 0.5),
        'w_na_out': nrm((L, NA_DIM, D_MODEL), NA_DIM ** -0.5),
        'w_o': nrm((L, D_MODEL, D_MODEL), D_MODEL ** -0.5),
        'b_o': nrm((L, D_MODEL), 0.01),
        'w_ff1': nrm((L, D_MODEL, D_FF), D_MODEL ** -0.5),
        'b_ff1': nrm((L, D_FF), 0.01),
        'w_ff2': nrm((L, D_FF, D_MODEL), D_FF ** -0.5),
        'b_ff2': nrm((L, D_MODEL), 0.01),
        'final_g': 1.0 + nrm((D_MODEL,), 0.05),
    }


def reference(x, c, ctx, c_ctx, w_mod, b_mod, norm1_g, norm2_g, w_in, b_in, conv_w, conv_b,
              conv_ln_g, conv_ln_b, w_conv_out, sgu_ln_g, sgu_ln_b, sgu_w, sgu_b, w_sgu_out,
              na_rpb, w_na_out, w_o, b_o, w_ff1, b_ff1, w_ff2, b_ff2, final_g):
    silu_c = jax.nn.silu(c)
    silu_cc = jax.nn.silu(c_ctx)[None]
    xl, xc = x, ctx
    for l in range(DEPTH):
        last = l == DEPTH - 1
        mod_l = jnp.split(silu_c @ w_mod[l] + b_mod[l], 6, axis=-1)
        mod_c = jnp.split(silu_cc @ w_mod[l] + b_mod[l], 6, axis=-1)
        branch_params = (conv_w[l], conv_b[l], conv_ln_g[l], conv_ln_b[l], w_conv_out[l],
                         sgu_ln_g[l], sgu_ln_b[l], sgu_w[l], sgu_b[l], w_sgu_out[l],
                         w_na_out[l], w_o[l], b_o[l])
        hl = modulate(rms_norm(xl, norm1_g[l]), mod_l[0], mod_l[1])
        hc = modulate(rms_norm(xc, norm1_g[l]), mod_c[0], mod_c[1])
        zl = hl @ w_in[l] + b_in[l]
        if last:
            zkv = hc @ w_in[l, :, K_OFF:G_OFF] + b_in[l, K_OFF:G_OFF]
            kc, vc = split_heads(zkv[..., :NA_DIM]), split_heads(zkv[..., NA_DIM:])
        else:
            zc = hc @ w_in[l] + b_in[l]
            kc, vc = split_heads(zc[..., K_OFF:V_OFF]), split_heads(zc[..., V_OFF:G_OFF])
        y_na_l = neighbourhood_attention(split_heads(zl[..., Q_OFF:K_OFF]), split_heads(zl[..., K_OFF:V_OFF]),
                                         split_heads(zl[..., V_OFF:G_OFF]), kc, vc, na_rpb[l])
        xl = xl + mod_l[2][:, None, :] * mixer_output(zl, y_na_l, *branch_params)
        hl2 = modulate(rms_norm(xl, norm2_g[l]), mod_l[3], mod_l[4])
        xl = xl + mod_l[5][:, None, :] * sq_relu_ffn(hl2, w_ff1[l], b_ff1[l], w_ff2[l], b_ff2[l])
        if not last:
            y_na_c = context_attention(split_heads(zc[..., Q_OFF:K_OFF]), kc, vc)
            xc = xc + mod_c[2][:, None, :] * mixer_output(zc, y_na_c, *branch_params)
            hc2 = modulate(rms_norm(xc, norm2_g[l]), mod_c[3], mod_c[4])
            xc = xc + mod_c[5][:, None, :] * sq_relu_ffn(hc2, w_ff1[l], b_ff1[l], w_ff2[l], b_ff2[l])
    return rms_norm(xl, final_g)
```

```python
import numpy as np
from contextlib import ExitStack
import concourse.bass as bass
import concourse.mybir as mybir
from concourse.bass_utils import run_bass_kernel_spmd

F32 = mybir.dt.float32
BF16 = mybir.dt.bfloat16
I32 = mybir.dt.int32
AF = mybir.ActivationFunctionType
ALU = mybir.AluOpType

GRID_W = 64
CTX_LEN = 256
SEQ = 4096
BATCH = 2
EPS = 1e-6
NEG = -1e30
CONV_WIDTH = 31
NA_KH, NA_KW = 8, 16
TOWN, TCTX, T, THALO, TALL = 1024, 256, 1280, 512, 1792
TQ = 64
BLKS = [(0, 512), (512, 512), (1024, 256)]
NCOL = 256
NSLOT = 3


class Cfg:
    def __init__(self, D, L):
        self.D = D
        self.L = L
        self.KC = D // 128
        self.CD = D // 2
        self.CC = self.CD // 128
        self.NH = self.CD // 64
        self.HP = self.NH // 2
        self.DFF = 4 * D
        self.FC = self.DFF // 128
        self.IN = 2 * self.CD + 2 * self.CD + 3 * self.CD + 3 * D
        self.A1, self.A2 = 0, self.CD
        self.SU, self.SV = 2 * self.CD, 3 * self.CD
        self.Q, self.K, self.V = 4 * self.CD, 5 * self.CD, 6 * self.CD
        self.G = 7 * self.CD
        o = 0
        self.voff = {}
        for name, n in [("n1g", self.KC), ("n2g", self.KC), ("b_in", self.IN // 128), ("conv_w", self.CC * 31),
                        ("conv_b", self.CC), ("cln_g", self.CC), ("cln_b", self.CC), ("sln_g", self.CC),
                        ("sln_b", self.CC), ("b_o", self.KC), ("b_ff1", self.FC), ("b_ff2", self.KC),
                        ("b_mod", 6 * self.KC), ("b_mod_next", 6 * self.KC)]:
            self.voff[name] = (o, n)
            o += n
        self.NV = o


def fm(v):
    v = np.asarray(v, np.float32)
    return np.ascontiguousarray(v.reshape(-1, 128).T)


def pack_vecs(cfg, inp, l):
    out = np.zeros((128, cfg.NV), np.float32)

    def put(name, arr):
        o, n = cfg.voff[name]
        assert arr.shape == (128, n), (name, arr.shape, n)
        out[:, o:o + n] = arr

    put("n1g", fm(inp["norm1_g"][l]))
    put("n2g", fm(inp["norm2_g"][l]))
    put("b_in", fm(inp["b_in"][l]))
    cw = np.asarray(inp["conv_w"][l], np.float32)
    put("conv_w", np.ascontiguousarray(cw.T.reshape(cfg.CC, 128, 31).transpose(1, 0, 2).reshape(128, cfg.CC * 31)))
    put("conv_b", fm(inp["conv_b"][l]))
    put("cln_g", fm(inp["conv_ln_g"][l]))
    put("cln_b", fm(inp["conv_ln_b"][l]))
    put("sln_g", fm(inp["sgu_ln_g"][l]))
    put("sln_b", fm(inp["sgu_ln_b"][l]))
    put("b_o", fm(inp["b_o"][l]))
    put("b_ff1", fm(inp["b_ff1"][l]))
    put("b_ff2", fm(inp["b_ff2"][l]))
    put("b_mod", fm(inp["b_mod"][l]))
    if l + 1 < np.asarray(inp["b_mod"]).shape[0]:
        put("b_mod_next", fm(inp["b_mod"][l + 1]))
    return out


def build_tab(cfg, rpb):
    rpb = np.asarray(rpb, np.float32)
    c = np.arange(64)
    w = np.arange(64)
    ws = np.clip(c - NA_KW // 2, 0, GRID_W - NA_KW)
    colvalid = (w[:, None] >= ws[None, :]) & (w[:, None] < ws[None, :] + NA_KW)
    dc = np.clip(w[:, None] - c[None, :] + NA_KW - 1, 0, 2 * NA_KW - 2)
    tab = np.full((2, 64, cfg.NH, 16, 64), NEG, np.float32)
    for i in range(2):
        for s in range(16):
            dr = s + i - 8
            if -7 <= dr <= 7:
                vals = rpb[:, dr + NA_KH - 1][:, dc]
                vals = np.where(colvalid[None], vals, np.float32(NEG))
                tab[i, :, :, s, :] = vals.transpose(1, 0, 2)
    return np.ascontiguousarray(tab.reshape(128, cfg.NH * 16 * 64))


def chunk_list(j):
    lo, hi = j // 2, (j + 7) // 2
    if j <= 1:
        lo, hi = 0, 5
    elif j <= 3:
        lo, hi = 1, 5
    elif j == 14:
        lo, hi = 6, 10
    elif j == 15:
        lo, hi = 6, 11
    return list(range(lo, hi + 1))


def build_rowmask(q):
    rm = np.full((2, 64, 16, 6), NEG, np.float32)
    rows = SEQ // GRID_W
    for j in range(16):
        r = 16 * q + j
        w0 = min(max(r - NA_KH // 2, 0), rows - NA_KH)
        for ci, P in enumerate(chunk_list(j)):
            for i in range(2):
                kr = 16 * q - 4 + 2 * P + i
                if 0 <= kr < rows and w0 <= kr < w0 + NA_KH:
                    rm[i, :, j, ci] = 0.0
    return np.ascontiguousarray(rm.reshape(128, 16 * 6))


def pair_tok(P):
    if P < 2:
        return 1280 + P * 128
    if P < 10:
        return (P - 2) * 128
    return 1536 + (P - 10) * 128


class EngState:
    def __init__(self, key, h, sem):
        self.key, self.h, self.sem = key, h, sem
        self.count = 0
        self.waited = {}


class BufState:
    __slots__ = ("lw", "rd")

    def __init__(self):
        self.lw = None
        self.rd = {}


class K:
    def __init__(self, nc, es, cfg):
        self.nc, self.es, self.cfg = nc, es, cfg
        self.eng = {}
        for key, h in [("pe", nc.tensor), ("act", nc.scalar), ("dve", nc.vector), ("pool", nc.gpsimd),
                       ("sp", nc.sync)]:
            self.eng[key] = EngState(key, h, es.enter_context(nc.semaphore("s_" + key)))
        self.bufs = {}
        self.dsems = [es.enter_context(nc.semaphore("s_d%d" % i)) for i in range(8)]
        self.dsem_val = [0] * 8
        self.dsem_rr = 0
        self.wsem = [es.enter_context(nc.semaphore("s_w%d" % i)) for i in range(NSLOT)]
        self.wsem_val = [0] * NSLOT

    def _deps(self, r, w):
        deps = {}

        def add(tok):
            if tok is None:
                return
            key, sem, val = tok
            if key not in deps or deps[key][1] < val:
                deps[key] = (sem, val)

        for k in r:
            st = self.bufs.get(k)
            if st is not None:
                add(st.lw)
        for k in w:
            st = self.bufs.get(k)
            if st is not None:
                add(st.lw)
                for d in st.rd.values():
                    add(d)
        return deps

    def _wait(self, E, deps, skip_self):
        for key, (sem, val) in deps.items():
            if skip_self and key == E.key:
                continue
            if E.waited.get(key, 0) < val:
                E.h.wait_ge(sem, val)
                E.waited[key] = val

    def _record(self, tok, r, w):
        for k in w:
            st = self.bufs.get(k)
            if st is None:
                st = self.bufs[k] = BufState()
            st.lw = tok
            st.rd = {}
        for k in r:
            st = self.bufs.get(k)
            if st is None:
                st = self.bufs[k] = BufState()
            st.rd[tok[0]] = tok

    def op(self, e, fn, r=(), w=()):
        E = self.eng[e]
        self._wait(E, self._deps(r, w), skip_self=(e == "pe"))
        ins = fn()
        E.count += 1
        ins.then_inc(E.sem, 1)
        self._record((E.key, E.sem, E.count), r, w)

    def dma(self, q, out, in_, r=(), w=(), wslot=None):
        E = self.eng[q]
        self._wait(E, self._deps(r, w), skip_self=True)
        if wslot is not None:
            sem = self.wsem[wslot]
            self.wsem_val[wslot] += 16
            val = self.wsem_val[wslot]
            key = "w%d" % wslot
        else:
            i = self.dsem_rr
            self.dsem_rr = (i + 1) % len(self.dsems)
            sem = self.dsems[i]
            key = "d%d" % i
            if self.dsem_val[i] > 0 and E.waited.get(key, 0) < self.dsem_val[i]:
                E.h.wait_ge(sem, self.dsem_val[i])
                E.waited[key] = self.dsem_val[i]
            self.dsem_val[i] += 16
            val = self.dsem_val[i]
        E.h.dma_start(out=out, in_=in_).then_inc(sem, 16)
        self._record((key, sem, val), r, w)

    def fence(self):
        names = ["pe", "act", "dve", "sp"]
        for a in names:
            A = self.eng[a]
            for b in ["pe", "act", "dve"]:
                if a == b:
                    continue
                B = self.eng[b]
                if B.count > 0 and A.waited.get(b, 0) < B.count:
                    A.h.wait_ge(B.sem, B.count)
                    A.waited[b] = B.count
            if a != "sp":
                for i, s in enumerate(self.dsems):
                    key = "d%d" % i
                    if self.dsem_val[i] > 0 and A.waited.get(key, 0) < self.dsem_val[i]:
                        A.h.wait_ge(s, self.dsem_val[i])
                        A.waited[key] = self.dsem_val[i]


class Region:
    def __init__(self, t, nbytes):
        self.t = t
        self.f32 = t
        self.b16 = t.bitcast(BF16)
        self.nbytes = nbytes
        self.off = 0

    def reset(self, off=0):
        self.off = off

    def alloc(self, dtype, shape):
        n = int(np.prod(shape))
        esz = 4 if dtype == F32 else 2
        self.off = (self.off + 31) // 32 * 32
        o = self.off
        assert o + n * esz <= self.nbytes, ("region overflow", o, n * esz, self.nbytes)
        self.off = o + n * esz
        base = self.f32 if dtype == F32 else self.b16
        ap = base[:, o // esz:o // esz + n]
        if len(shape) == 2:
            ap = ap.rearrange("p (a b) -> p a b", a=shape[0])
        elif len(shape) == 3:
            ap = ap.rearrange("p (a b c) -> p a b c", a=shape[0], b=shape[1])
        return ap


def build_program(cfg, layers, final_norm, first_from_input=True, debug=False, mod_in=False, mod_next=False):
    nc = bass.Bass("TRN2", target_bir_lowering=False)
    D, KC, CD, CC, HP, FC = cfg.D, cfg.KC, cfg.CD, cfg.CC, cfg.HP, cfg.FC
    NL = len(layers)

    def din(name, shape, dt=F32):
        return nc.dram_tensor(name, list(shape), dt, kind="ExternalInput").ap()

    xT_d = din("xT", [D, T])
    xhT_d = din("xhT", [D, THALO])
    cT_d = din("cT", [128, KC * 2])
    vecs_d = din("vecs", [NL * 128, cfg.NV])
    fg_d = din("final_g", [128, KC])
    ident_d = din("ident", [128, 128])
    rm2_d = din("rm2", [2, 96])
    qmask_d = din("qmask", [128, 4])
    sel_d = din("sel", [2, 128])
    halomask_d = din("halomask", [128, 2])
    tab_d = din("tab", [NL * 128, cfg.NH * 1024])
    wsT_d = din("wsT", [NL * CC * 128, 128])
    bsb_d = din("bsb", [NL * 128, CC * 128])
    w_mod_d = din("w_mod", [NL * D, 6 * D]) if not mod_in else None
    w_modn_d = din("w_mod_next", [D, 6 * D]) if mod_next else None
    modin_d = din("modv_in", [128, 12 * KC]) if mod_in else None
    modout_d = nc.dram_tensor("modv_out", [128, 12 * KC], F32, kind="ExternalOutput").ap() if mod_next else None
    w_in_d = din("w_in", [NL * D, cfg.IN])
    w_co_d = din("w_conv_out", [NL * CD, D])
    w_so_d = din("w_sgu_out", [NL * CD, D])
    w_no_d = din("w_na_out", [NL * CD, D])
    w_o_d = din("w_o", [NL * D, D])
    w_f1_d = din("w_ff1", [NL * D, cfg.DFF])
    w_f2_d = din("w_ff2", [NL * cfg.DFF, D])
    if final_norm:
        out_d = nc.dram_tensor("outT", [D, TOWN], F32, kind="ExternalOutput").ap()
    else:
        out_d = nc.dram_tensor("outT", [D, T], F32, kind="ExternalOutput").ap()
    xsp_d = nc.dram_tensor("xspill", [D, T], F32, kind="Internal").ap()
    dbg = {}

    with ExitStack() as es:
        k = K(nc, es, cfg)
        R1B = KC * T * 4
        R2B = max(KC * T * 2, 40960)
        XB = 13 * 1024
        hT_t = es.enter_context(nc.sbuf_tensor("hT", [128, KC * T], BF16))
        R1 = Region(es.enter_context(nc.sbuf_tensor("R1", [128, R1B // 4], F32)), R1B)
        R2 = Region(es.enter_context(nc.sbuf_tensor("R2", [128, R2B // 4], F32)), R2B)
        RX = Region(es.enter_context(nc.sbuf_tensor("RX", [128, XB // 4], F32)), XB)
        WS_t = es.enter_context(nc.sbuf_tensor("WS", [128, NSLOT * KC * NCOL], BF16))
        NM = cfg.NV + 34 * KC + 1100
        misc = Region(es.enter_context(nc.sbuf_tensor("misc", [128, NM + 256], F32)), (NM + 256) * 4)
        psum = [es.enter_context(nc.psum_tensor("ps%d" % i, [128, 512], F32)) for i in range(8)]
        ps_state = {"gen": list(range(8)), "rr": 0, "acc": [], "arr": 0}

        def ps_pools(gen, acc):
            ps_state["gen"], ps_state["acc"], ps_state["rr"], ps_state["arr"] = gen, acc, 0, 0

        def ps_get(kind="gen"):
            if kind == "gen":
                lst = ps_state["gen"]
                i = lst[ps_state["rr"] % len(lst)]
                ps_state["rr"] += 1
            else:
                lst = ps_state["acc"]
                i = lst[ps_state["arr"] % len(lst)]
                ps_state["arr"] += 1
            return psum[i], ("ps", i)

        def dump(name, ap, li_only=0, li=0):
            if not debug or li != li_only:
                return
            k.fence()
            shp = [128, int(np.prod(ap.shape[1:]))]
            dt = ap.dtype
            d = nc.dram_tensor("dbg_" + name, shp, dt, kind="ExternalOutput").ap()
            flat = ap
            if len(ap.shape) == 3:
                flat = ap.rearrange("p a b -> p (a b)")
            elif len(ap.shape) == 4:
                flat = ap.rearrange("p a b c -> p (a b c)")
            k.dma("sp", d[:, :], flat, w=[("dbg", name)])
            dbg[name] = shp

        hT = hT_t[:, :].rearrange("p (c t) -> p c t", c=KC)
        WS = WS_t[:, :].rearrange("p (s c n) -> p s c n", s=NSLOT, c=KC)

        vecs = misc.alloc(F32, [cfg.NV])
        sc = misc.alloc(F32, [KC, 2])
        modv = misc.alloc(F32, [6 * KC, 2])
        modvn = misc.alloc(F32, [6 * KC, 2])
        Amod = misc.alloc(F32, [2, KC, 2])
        bog = misc.alloc(F32, [2, KC, 2])
        fgv = misc.alloc(F32, [KC])
        halomask = misc.alloc(F32, [2])
        onesD = misc.alloc(BF16, [128])
        ident = misc.alloc(BF16, [128])
        onesC = misc.alloc(BF16, [128])
        ones1 = misc.alloc(BF16, [128])
        identf = misc.alloc(F32, [128])
        scb = misc.alloc(BF16, [KC, 2])
        rm2f = misc.alloc(F32, [96])
        qmask = misc.alloc(F32, [4])
        rm2b = misc.alloc(BF16, [96])
        self_f = misc.alloc(F32, [128])
        selb = misc.alloc(BF16, [128])
        modrow = misc.alloc(F32, [256])

        def vec(name, i=None, n=1):
            o, cnt = cfg.voff[name]
            if i is None:
                return vecs[:, o:o + cnt]
            return vecs[:, o + i:o + i + n]

        def mod(sec, kc, r):
            return modv[:, sec * KC + kc, r:r + 1]

        k.op("dve", lambda: nc.vector.memset(onesD, 1.0 / D), w=[("onesD",)])
        k.op("dve", lambda: nc.vector.memset(onesC, 1.0 / CD), w=[("onesC",)])
        k.op("dve", lambda: nc.vector.memset(ones1, 1.0), w=[("ones1",)])
        k.dma("sp", identf, ident_d[:, :], w=[("identf",)])
        k.op("dve", lambda: nc.vector.tensor_copy(out=ident, in_=identf), r=[("identf",)], w=[("ident",)])
        k.dma("sp", sc.rearrange("p a b -> p (a b)"), cT_d[:, :], w=[("sc",)])
        k.dma("sp", fgv, fg_d[:, :], w=[("fgv",)])
        k.dma("sp", halomask, halomask_d[:, :], w=[("halomask",)])
        k.dma("sp", rm2f[0:2, :], rm2_d[:, :], w=[("rm2f",)])
        k.dma("sp", qmask, qmask_d[:, :], w=[("qmask",)])
        k.dma("sp", self_f[0:2, :], sel_d[:, :], w=[("self",)])
        k.op("dve", lambda: nc.vector.tensor_copy(out=rm2b[0:2, :], in_=rm2f[0:2, :]), r=[("rm2f",)], w=[("rm2b",)])
        k.op("dve", lambda: nc.vector.tensor_copy(out=selb[0:2, :], in_=self_f[0:2, :]), r=[("self",)], w=[("selb",)])
        k.op("act", lambda: nc.scalar.activation(out=sc, in_=sc, func=AF.Silu), r=[("sc",)], w=[("sc",)])
        k.op("dve", lambda: nc.vector.tensor_copy(out=scb, in_=sc), r=[("sc",)], w=[("scb",)])

        wq = []
        wq_issued = [0]
        wq_used = [0]

        def w_issue(i):
            if i >= len(wq) or i < wq_issued[0]:
                return
            assert i == wq_issued[0]
            s_ = i % NSLOT
            for (src, kc0, c0) in wq[i]:
                rows, cols = src.shape
                nk = rows // 128
                k.dma("pool", WS[:, s_, kc0:kc0 + nk, c0:c0 + cols],
                      src.rearrange("(c p) n -> p c n", p=128), w=[("ws", s_)], wslot=s_)
            wq_issued[0] += 1

        def w_next():
            i = wq_used[0]
            wq_used[0] += 1
            for jj in range(wq_issued[0], i + NSLOT):
                w_issue(jj)
            s_ = i % NSLOT
            return WS[:, s_], ("ws", s_)

        def declare_stream():
            for li, l in enumerate(layers):
                r0 = li * D
                rc = li * CD
                mpos = [0]

                def modq(nb):
                    for _ in range(nb):
                        if mpos[0] >= 6 * KC:
                            return
                        c0 = mpos[0] * 128
                        wq.append([(w_mod_d[r0:r0 + D, c0:c0 + 256], 0, 0)])
                        mpos[0] += 2

                if mod_in:
                    mpos[0] = 6 * KC
                mnpos = [0 if mod_next else 6 * KC]

                def modnq(nb):
                    for _ in range(nb):
                        if mnpos[0] >= 6 * KC:
                            return
                        c0 = mnpos[0] * 128
                        wq.append([(w_modn_d[0:D, c0:c0 + 256], 0, 0)])
                        mnpos[0] += 2

                modq(KC)
                for j in range(0, CC, 2):
                    wq.append([(w_in_d[r0:r0 + D, cfg.SV + j * 128: cfg.SV + (j + 2) * 128], 0, 0)])
                    modq(2)
                    if mod_in:
                        modnq(3)
                for j in range(0, CC, 2):
                    wq.append([(w_in_d[r0:r0 + D, cfg.SU + j * 128: cfg.SU + (j + 2) * 128], 0, 0)])
                    modq(2)
                    if mod_in:
                        modnq(3)
                for j in range(CC):
                    wq.append([(w_in_d[r0:r0 + D, cfg.A1 + j * 128: cfg.A1 + (j + 1) * 128], 0, 0),
                               (w_in_d[r0:r0 + D, cfg.A2 + j * 128: cfg.A2 + (j + 1) * 128], 0, 128)])
                    modq(2)
                    if mod_in:
                        modnq(3)
                assert mpos[0] == 6 * KC
                for j in range(HP):
                    wq.append([(w_in_d[r0:r0 + D, cfg.Q + j * 128: cfg.Q + (j + 1) * 128], 0, 0),
                               (w_in_d[r0:r0 + D, cfg.K + j * 128: cfg.K + (j + 1) * 128], 0, 128)])
                    wq.append([(w_in_d[r0:r0 + D, cfg.V + j * 128: cfg.V + (j + 1) * 128], 0, 0)])
                bo = [w_co_d, w_so_d, w_no_d]
                for f in range(KC):
                    for b in range(3):
                        wq.append([(w_in_d[r0:r0 + D, cfg.G + b * D + f * 128: cfg.G + b * D + (f + 1) * 128], 0, 0),
                                   (bo[b][rc:rc + CD, f * 128:(f + 1) * 128], 0, 128)])
                    if not mod_in:
                        modnq(1)
                for o in range(0, KC, 2):
                    wq.append([(w_o_d[r0:r0 + D, o * 128:(o + 2) * 128], 0, 0)])
                dcnt = 0
                for g in range(FC // KC):
                    for j in range(0, KC, 2):
                        c0 = (g * KC + j) * 128
                        wq.append([(w_f1_d[r0:r0 + D, c0:c0 + 256], 0, 0)])
                        dcnt += 1
                        if not mod_in and dcnt % 2 == 0:
                            modnq(1)
                    for o in range(0, KC, 2):
                        rr = li * cfg.DFF + g * D
                        wq.append([(w_f2_d[rr:rr + D, o * 128:(o + 2) * 128], 0, 0)])
                        dcnt += 1
                        if not mod_in and dcnt % 2 == 0:
                            modnq(1)
                assert mnpos[0] == 6 * KC, mnpos

        declare_stream()

        def proj(ws, wkey, col0, nkc, rhs_fn, rkeys, n):
            pt, pk = ps_get()

            def emit():
                ins = None
                for kc in range(nkc):
                    ins = nc.tensor.matmul(pt[:, 0:n], lhsT=ws[:, kc, col0:col0 + 128], rhs=rhs_fn(kc),
                                           start=(kc == 0), stop=(kc == nkc - 1))
                return ins

            k.op("pe", emit, r=[wkey] + list(rkeys), w=[pk])
            return pt, pk

        def rms_mod(x_fn, xkeys, n, out_fn, okeys_fn, A_fn, S_fn, scratch, mkeys):
            sq, rstd, tmp = scratch
            pt, pk = ps_get()
            for kc in range(KC):
                sqb = sq[kc % 2]
                k.op("act", lambda: nc.scalar.activation(out=sqb[:, 0:n], in_=x_fn(kc), func=AF.Square),
                     r=xkeys(kc), w=[("sq", kc % 2)])
                k.op("pe", lambda: nc.tensor.matmul(pt[:, 0:n], lhsT=onesD, rhs=sqb[:, 0:n],
                                                    start=(kc == 0), stop=(kc == KC - 1)),
                     r=[("sq", kc % 2), ("onesD",)], w=[pk])
            k.op("act", lambda: nc.scalar.activation(out=rstd[:, 0:n], in_=pt[:, 0:n], func=AF.Sqrt, bias=EPS,
                                                     scale=1.0), r=[pk], w=[("rstd",)])
            k.op("dve", lambda: nc.vector.reciprocal(out=rstd[:, 0:n], in_=rstd[:, 0:n]), r=[("rstd",)],
                 w=[("rstd",)])
            for kc in range(KC):
                tb = tmp[kc % 2]
                k.op("dve", lambda: nc.vector.scalar_tensor_tensor(out=tb[:, 0:n], in0=x_fn(kc), scalar=A_fn(kc),
                                                                   in1=rstd[:, 0:n], op0=ALU.mult, op1=ALU.mult),
                     r=xkeys(kc) + [("rstd",)] + mkeys, w=[("ntmp", kc % 2)])
                k.op("act", lambda: nc.scalar.activation(out=out_fn(kc), in_=tb[:, 0:n], func=AF.Identity,
                                                         bias=S_fn(kc), scale=1.0),
                     r=[("ntmp", kc % 2)] + mkeys, w=okeys_fn(kc))

        def ln_fm(buf, bkey, nch, gname, bname, func, scratch_region, blks):
            mean = scratch_region.alloc(F32, [T])
            rstd = scratch_region.alloc(F32, [T])
            sqs = [scratch_region.alloc(BF16, [512]) for _ in range(2)]
            msq = scratch_region.alloc(F32, [512])
            for bi, (t0, n) in enumerate(blks):
                p1, k1 = ps_get()
                p2, k2 = ps_get()
                for c in range(nch):
                    k.op("pe", lambda: nc.tensor.matmul(p1[:, 0:n], lhsT=onesC, rhs=buf[:, c, t0:t0 + n],
                                                        start=(c == 0), stop=(c == nch - 1)),
                         r=[(bkey, c, bi), ("onesC",)], w=[k1])
                for c in range(nch):
                    sb = sqs[c % 2]
                    k.op("act", lambda: nc.scalar.activation(out=sb[:, 0:n], in_=buf[:, c, t0:t0 + n], func=AF.Square),
                         r=[(bkey, c, bi)], w=[("lnsq", c % 2)])
                    k.op("pe", lambda: nc.tensor.matmul(p2[:, 0:n], lhsT=onesC, rhs=sb[:, 0:n],
                                                        start=(c == 0), stop=(c == nch - 1)),
                         r=[("lnsq", c % 2), ("onesC",)], w=[k2])
                k.op("act", lambda: nc.scalar.copy(out=mean[:, t0:t0 + n], in_=p1[:, 0:n]), r=[k1],
                     w=[("lnmean", bi)])
                k.op("dve", lambda: nc.vector.tensor_tensor(out=msq[:, 0:n], in0=mean[:, t0:t0 + n],
                                                            in1=mean[:, t0:t0 + n], op=ALU.mult),
                     r=[("lnmean", bi)], w=[("lnmsq",)])
                k.op("dve", lambda: nc.vector.tensor_tensor(out=msq[:, 0:n], in0=p2[:, 0:n], in1=msq[:, 0:n],
                                                            op=ALU.subtract),
                     r=[k2, ("lnmsq",)], w=[("lnmsq",)])
                k.op("act", lambda: nc.scalar.activation(out=rstd[:, t0:t0 + n], in_=msq[:, 0:n], func=AF.Sqrt,
                                                         bias=EPS, scale=1.0), r=[("lnmsq",)], w=[("lnrstd", bi)])
                k.op("dve", lambda: nc.vector.reciprocal(out=rstd[:, t0:t0 + n], in_=rstd[:, t0:t0 + n]),
                     r=[("lnrstd", bi)], w=[("lnrstd", bi)])
            tmps = [scratch_region.alloc(F32, [512]) for _ in range(2)]
            i = 0
            for c in range(nch):
                for bi, (t0, n) in enumerate(blks):
                    tb = tmps[i % 2]
                    k.op("dve", lambda: nc.vector.tensor_tensor(out=tb[:, 0:n], in0=buf[:, c, t0:t0 + n],
                                                                in1=mean[:, t0:t0 + n], op=ALU.subtract),
                         r=[(bkey, c, bi), ("lnmean", bi)], w=[("lntmp", i % 2)])
                    k.op("dve", lambda: nc.vector.tensor_tensor(out=tb[:, 0:n], in0=tb[:, 0:n],
                                                                in1=rstd[:, t0:t0 + n], op=ALU.mult),
                         r=[("lntmp", i % 2), ("lnrstd", bi)], w=[("lntmp", i % 2)])
                    k.op("act", lambda: nc.scalar.activation(out=buf[:, c, t0:t0 + n], in_=tb[:, 0:n], func=func,
                                                             bias=vec(bname, c), scale=vec(gname, c)),
                         r=[("lntmp", i % 2), ("vecs",)], w=[(bkey, c, bi)])
                    i += 1

        xT = R1.f32[:, 0:KC * T].rearrange("p (c t) -> p c t", c=KC)
        xkey = lambda kc, bi: ("x", kc, bi)
        for kc in range(KC):
            k.dma("sp", xT[:, kc, :], xT_d[kc * 128:(kc + 1) * 128, :], w=[xkey(kc, bi) for bi in range(3)])

        for li, l in enumerate(layers):
            last = final_norm and (li == NL - 1)
            r0 = li * D
            k.fence()
            R2.reset()
            RX.reset()
            k.dma("sp", vecs, vecs_d[li * 128:(li + 1) * 128, :], w=[("vecs",)])
            modpos = [6 * KC if mod_in else 0]
            modnpos = [0 if mod_next else 6 * KC]

            def mod_blocks(nb, nxt=False):
                pos = modnpos if nxt else modpos
                tgt = modvn if nxt else modv
                bname = "b_mod_next" if nxt else "b_mod"
                for _ in range(nb):
                    if pos[0] >= 6 * KC:
                        return
                    ws, wk = w_next()
                    j0 = pos[0]
                    pos[0] += 2
                    pt, pk = ps_get()

                    def emit():
                        ins = None
                        for kc in range(KC):
                            ins = nc.tensor.matmul(pt[0:2, 0:256], lhsT=scb[:, kc, :], rhs=ws[:, kc, 0:256],
                                                   start=(kc == 0), stop=(kc == KC - 1))
                        return ins

                    k.op("pe", emit, r=[wk, ("scb",)], w=[pk])
                    k.op("act", lambda: nc.scalar.copy(out=modrow[0:2, :], in_=pt[0:2, 0:256]), r=[pk],
                         w=[("modrow",)])
                    pt2, pk2 = ps_get()

                    def emit2():
                        ins = None
                        for cj in range(2):
                            ins = nc.tensor.matmul(pt2[:, 2 * cj:2 * cj + 2], lhsT=modrow[0:2, cj * 128:(cj + 1) * 128],
                                                   rhs=identf[0:2, 0:2], start=True, stop=True)
                        return ins

                    k.op("pe", emit2, r=[("modrow",), ("identf",)], w=[pk2])
                    sec = j0 // KC
                    k.op("dve", lambda: nc.vector.tensor_tensor(
                        out=tgt[:, j0:j0 + 2, :], in0=pt2[:, 0:4].rearrange("p (j r) -> p j r", r=2),
                        in1=vec(bname)[:, j0:j0 + 2].rearrange("p (j o) -> p j o", o=1).broadcast_to([128, 2, 2]),
                        op=ALU.add), r=[pk2, ("vecs",)], w=[("modn",) if nxt else ("mod", sec)])

            def mod_derive(wn):
                gname, sec = [("n1g", 1), ("n2g", 4)][wn]
                for r in range(2):
                    k.op("dve", lambda: nc.vector.scalar_tensor_tensor(
                        out=Amod[:, wn, :, r], in0=modv[:, sec * KC:(sec + 1) * KC, r], scalar=1.0, in1=vec(gname),
                        op0=ALU.add, op1=ALU.mult), r=[("mod", sec), ("vecs",)], w=[("amod", wn)])

            def bog_derive(wn):
                bname, sec = [("b_o", 2), ("b_ff2", 5)][wn]
                for r in range(2):
                    k.op("dve", lambda: nc.vector.tensor_tensor(out=bog[:, wn, :, r],
                                                                in0=modv[:, sec * KC:(sec + 1) * KC, r],
                                                                in1=vec(bname), op=ALU.mult),
                         r=[("mod", sec), ("vecs",)], w=[("bog", wn)])

            if mod_in:
                k.dma("sp", modv.rearrange("p a b -> p (a b)"), modin_d[:, :], w=[("mod", sec_) for sec_ in range(6)])
            mod_blocks(KC)
            mod_derive(0)

            dump("modv", modv, li=li)
            dump("amod", Amod, li=li)
            k.fence()
            R2.reset()
            h_halo = R2.alloc(BF16, [KC, THALO])
            xh_st = R2.alloc(F32, [KC, 256])
            RX.reset()
            scratch = ([RX.alloc(BF16, [512]) for _ in range(2)], RX.alloc(F32, [512]),
                       [RX.alloc(F32, [512]) for _ in range(2)])
            for bi, (t0, n) in enumerate(BLKS):
                r = 0 if bi < 2 else 1
                rms_mod(lambda kc: xT[:, kc, t0:t0 + n], lambda kc: [xkey(kc, bi)], n,
                        lambda kc: hT[:, kc, t0:t0 + n], lambda kc: [("h", kc, bi)],
                        lambda kc: Amod[:, 0, kc, r:r + 1], lambda kc: mod(0, kc, r), scratch, [("mod", 0), ("amod", 0)])
            for hh in range(2):
                src = xhT_d
                k.dma("sp", xh_st, src[:, hh * 256:(hh + 1) * 256].rearrange("(c p) t -> p c t", p=128),
                      w=[("xh",)])
                rms_mod(lambda kc: xh_st[:, kc, :], lambda kc: [("xh",)], 256,
                        lambda kc: h_halo[:, kc, hh * 256:(hh + 1) * 256], lambda kc: [("hh", kc)],
                        lambda kc: Amod[:, 0, kc, 0:1], lambda kc: mod(0, kc, 0), scratch, [("mod", 0), ("amod", 0)])
            for kc in range(KC):
                k.dma("sp", xsp_d[kc * 128:(kc + 1) * 128, :], xT[:, kc, :], r=[xkey(kc, bi) for bi in range(3)],
                      w=[("xsp", kc)])

            dump("tmp0", scratch[2][0], li=li)
            dump("tmp1", scratch[2][1], li=li)
            dump("rstd", scratch[1], li=li)
            dump("xhst", xh_st, li=li)
            dump("h1", hT, li=li)
            dump("hhalo", h_halo, li=li)
            R2.reset(KC * THALO * 2)
            RX.reset()
            R1b = R1.b16
            convin = R1b[:, 0:CC * T].rearrange("p (c t) -> p c t", c=CC)
            sguin = R1b[:, CC * T:2 * CC * T].rearrange("p (c t) -> p c t", c=CC)
            att = R1b[:, 2 * CC * T:3 * CC * T].rearrange("p (c t) -> p c t", c=CC)
            vact = R1b[:, 3 * CC * T:4 * CC * T].rearrange("p (c t) -> p c t", c=CC)
            hkeys = lambda bi: [("h", kc, bi) for kc in range(KC)]
            ABLK = BLKS[:2] if last else BLKS

            for j0 in range(0, CC, 2):
                ws, wk = w_next()
                for cj in range(2):
                    j = j0 + cj
                    for bi, (t0, n) in enumerate(ABLK):
                        pt, pk = proj(ws, wk, cj * 128, KC, lambda kc: hT[:, kc, t0:t0 + n], hkeys(bi), n)
                        k.op("act", lambda: nc.scalar.activation(out=vact[:, j, t0:t0 + n], in_=pt[:, 0:n],
                                                                 func=AF.Gelu_apprx_tanh,
                                                                 bias=vec("b_in", cfg.SV // 128 + j), scale=1.0),
                             r=[pk, ("vecs",)],
                             w=[("vact", j, bi)] + [xkey((3 * CC + j) // 2, b_) for b_ in range(3)])
                mod_blocks(2)
                if mod_in:
                    mod_blocks(3, nxt=True)
            k.fence()
            ln_fm(vact, "vact", CC, "sln_g", "sln_b", AF.Identity, R2, ABLK)

            dump("vact", vact, li=li)
            k.fence()
            R2.reset(KC * THALO * 2)
            wsT = R2.alloc(BF16, [CC, 128])
            bsb = R2.alloc(F32, [CC, 128])
            uT = [R2.alloc(BF16, [T]) for _ in range(2)]
            vn = [R2.alloc(BF16, [10, 128]) for _ in range(2)]
            RX.reset()
            sgt = [RX.alloc(F32, [512]) for _ in range(2)]
            wsTf = R2.alloc(F32, [CC, 128])
            k.dma("sp", wsTf, wsT_d[li * CC * 128:(li + 1) * CC * 128, :].rearrange("(g q) p -> q g p", q=128),
                  w=[("wsTf",)])
            k.op("act", lambda: nc.scalar.copy(out=wsT, in_=wsTf), r=[("wsTf",)], w=[("wsT",)])
            k.dma("sp", bsb.rearrange("p g n -> p (g n)"), bsb_d[li * 128:(li + 1) * 128, :], w=[("bsb",)])
            i2 = 0
            for j0 in range(0, CC, 2):
                ws, wk = w_next()
                for cj in range(2):
                    g = j0 + cj
                    ub = uT[g % 2]
                    vb = vn[g % 2]
                    for bi, (t0, n) in enumerate(ABLK):
                        pt, pk = proj(ws, wk, cj * 128, KC, lambda kc: hT[:, kc, t0:t0 + n], hkeys(bi), n)
                        k.op("act", lambda: nc.scalar.activation(out=ub[:, t0:t0 + n], in_=pt[:, 0:n],
                                                                 func=AF.Gelu_apprx_tanh,
                                                                 bias=vec("b_in", cfg.SU // 128 + g), scale=1.0),
                             r=[pk, ("vecs",)], w=[("uT", g % 2, bi)])
                    for bi, (t0, n) in enumerate(ABLK):
                        nt = n // 128
                        pt, pk = ps_get()
                        ptb = pt.bitcast(BF16)

                        def emit():
                            ins = None
                            for tt in range(nt):
                                ins = nc.tensor.transpose(ptb[:, tt * 128:(tt + 1) * 128],
                                                          vact[:, g, t0 + tt * 128:t0 + (tt + 1) * 128], ident)
                            return ins

                        k.op("pe", emit, r=[("vact", g, bi), ("ident",)], w=[pk])
                        k.op("dve", lambda: nc.vector.tensor_copy(
                            out=vb[:, t0 // 128:t0 // 128 + nt, :],
                            in_=ptb[:, 0:n].rearrange("p (a b) -> p a b", b=128)),
                            r=[pk], w=[("vn", g % 2, bi)])
                    for bi, (t0, n) in enumerate(ABLK):
                        nt = n // 128
                        pt, pk = ps_get()

                        def emit():
                            ins = None
                            for tt in range(nt):
                                ins = nc.tensor.matmul(pt[:, tt * 128:(tt + 1) * 128], lhsT=vb[:, t0 // 128 + tt, :],
                                                       rhs=wsT[:, g, :], start=True, stop=True)
                            return ins

                        k.op("pe", emit, r=[("vn", g % 2, bi), ("wsT",)], w=[pk])
                        sb = sgt[i2 % 2]
                        k.op("dve", lambda: nc.vector.tensor_tensor(
                            out=sb[:, 0:n].rearrange("p (a b) -> p a b", b=128),
                            in0=pt[:, 0:n].rearrange("p (a b) -> p a b", b=128),
                            in1=bsb[:, g:g + 1, :].broadcast_to([128, nt, 128]), op=ALU.add),
                            r=[pk, ("bsb",)], w=[("sgt", i2 % 2)])
                        k.op("dve", lambda: nc.vector.tensor_tensor(out=sguin[:, g, t0:t0 + n], in0=sb[:, 0:n],
                                                                    in1=ub[:, t0:t0 + n], op=ALU.mult),
                             r=[("sgt", i2 % 2), ("uT", g % 2, bi)], w=[("sguin", g, bi)])
                        i2 += 1
                mod_blocks(2)
                if mod_in:
                    mod_blocks(3, nxt=True)

            dump("sguin", sguin, li=li)
            k.fence()
            R2.reset(KC * THALO * 2)
            UBL = 15 + TOWN + 15
            UBC = 15 + TCTX + 15
            ubuf = [R2.alloc(BF16, [UBL + UBC]) for _ in range(2)]
            dg = R2.alloc(BF16, [CONV_WIDTH, 128])
            sgm = [R2.alloc(F32, [512]) for _ in range(2)]
            for u in ubuf:
                k.op("dve", lambda: nc.vector.memset(u, 0.0), w=[("ubuf", 0), ("ubuf", 1)])
            cblks = [(0, 512, 15), (512, 512, 15 + 512), (1024, 256, UBL + 15)]
            if last:
                cblks = cblks[:2]
            it = 0
            for j in range(CC):
                ws, wk = w_next()
                ub = ubuf[j % 2]
                ukey = ("ubuf", j % 2)
                for kk in range(CONV_WIDTH):
                    k.op("dve", lambda: nc.vector.tensor_scalar(out=dg[:, kk, :], in0=ident,
                                                                scalar1=vec("conv_w", j * 31 + kk), scalar2=None,
                                                                op0=ALU.mult),
                         r=[("ident",), ("vecs",)], w=[("dg",)])
                for (t0, n, uo), bi in zip(cblks, range(3)):
                    p1, k1 = proj(ws, wk, 0, KC, lambda kc: hT[:, kc, t0:t0 + n], hkeys(bi), n)
                    p2, k2 = proj(ws, wk, 128, KC, lambda kc: hT[:, kc, t0:t0 + n], hkeys(bi), n)
                    sb = sgm[it % 2]
                    k.op("act", lambda: nc.scalar.activation(out=sb[:, 0:n], in_=p2[:, 0:n], func=AF.Sigmoid,
                                                             bias=vec("b_in", cfg.A2 // 128 + j), scale=1.0),
                         r=[k2, ("vecs",)], w=[("sgm", it % 2)])
                    k.op("dve", lambda: nc.vector.scalar_tensor_tensor(
                        out=ub[:, uo:uo + n], in0=p1[:, 0:n], scalar=vec("b_in", cfg.A1 // 128 + j), in1=sb[:, 0:n],
                        op0=ALU.add, op1=ALU.mult), r=[k1, ("sgm", it % 2), ("vecs",)], w=[ukey])
                    it += 1
                p1, k1 = proj(ws, wk, 0, KC, lambda kc: h_halo[:, kc, 241:271], [("hh", kc) for kc in range(KC)], 30)
                p2, k2 = proj(ws, wk, 128, KC, lambda kc: h_halo[:, kc, 241:271], [("hh", kc) for kc in range(KC)], 30)
                sb = sgm[it % 2]
                k.op("act", lambda: nc.scalar.activation(out=sb[:, 0:30], in_=p2[:, 0:30], func=AF.Sigmoid,
                                                         bias=vec("b_in", cfg.A2 // 128 + j), scale=1.0),
                     r=[k2, ("vecs",)], w=[("sgm", it % 2)])
                k.op("dve", lambda: nc.vector.scalar_tensor_tensor(
                    out=sb[:, 0:30], in0=p1[:, 0:30], scalar=vec("b_in", cfg.A1 // 128 + j), in1=sb[:, 0:30],
                    op0=ALU.add, op1=ALU.mult), r=[k1, ("sgm", it % 2), ("vecs",)], w=[("sgm", it % 2)])
                k.op("dve", lambda: nc.vector.tensor_scalar(out=ub[:, 0:15], in0=sb[:, 0:15],
                                                            scalar1=halomask[:, 0:1], scalar2=None, op0=ALU.mult),
                     r=[("sgm", it % 2), ("halomask",)], w=[ukey])
                k.op("dve", lambda: nc.vector.tensor_scalar(out=ub[:, 15 + TOWN:UBL], in0=sb[:, 15:30],
                                                            scalar1=halomask[:, 1:2], scalar2=None, op0=ALU.mult),
                     r=[("sgm", it % 2), ("halomask",)], w=[ukey])
                it += 1
                for (o_in, t0, n, bi) in [(0, 0, 512, 0), (512, 512, 512, 1), (UBL, 1024, 256, 2)][:len(cblks)]:
                    pt, pk = ps_get()

                    def emit():
                        ins = None
                        for kk in range(CONV_WIDTH):
                            ins = nc.tensor.matmul(pt[:, 0:n], lhsT=dg[:, kk, :], rhs=ub[:, o_in + kk:o_in + kk + n],
                                                   start=(kk == 0), stop=(kk == CONV_WIDTH - 1))
                        return ins

                    k.op("pe", emit, r=[("dg",), ukey], w=[pk])
                    k.op("act", lambda: nc.scalar.activation(out=convin[:, j, t0:t0 + n], in_=pt[:, 0:n],
                                                             func=AF.Identity, bias=vec("conv_b", j), scale=1.0),
                         r=[pk, ("vecs",)], w=[("convin", j, bi)])
                mod_blocks(2)
                if mod_in:
                    mod_blocks(3, nxt=True)
            k.fence()
            R2.reset(KC * THALO * 2)
            ln_fm(convin, "convin", CC, "cln_g", "cln_b", AF.Silu, R2, ABLK)

            dump("convin", convin, li=li)
            k.fence()
            R2.reset(KC * THALO * 2)
            RX.reset()
            ps_pools([0, 1, 2, 3], [4, 5, 6, 7])
            tabb = R2.alloc(BF16, [2, 16, 64])
            QT = R2.alloc(BF16, [20, 128])
            KT = R2.alloc(BF16, [TALL])
            VT = R2.alloc(BF16, [TALL])
            Vt = R2.alloc(BF16, [14, 128])
            ET = [RX.alloc(BF16, [8, 128]) for _ in range(2)]
            tab = RX.alloc(F32, [16, 64])
            rs = RX.alloc(F32, [4, 128])
            k.op("dve", lambda: nc.vector.memset(QT, 0.0), w=[("QT",)])
            allblk = [(0, 512, hT, "h", 0), (512, 512, hT, "h", 1), (1024, 256, hT, "h", 2), (1280, 512, h_halo, "hh", None)]
            ie = 0
            nqrows = 16 if last else 20
            for hp in range(HP):
                ws, wk = w_next()
                for hh_ in range(2):
                    k.dma("sp", tab.rearrange("p b c -> p (b c)"),
                          tab_d[li * 128:(li + 1) * 128, (2 * hp + hh_) * 1024:(2 * hp + hh_ + 1) * 1024],
                          w=[("tab",)])
                    k.op("act", lambda: nc.scalar.copy(out=tabb[:, hh_, :, :], in_=tab), r=[("tab",)],
                         w=[("tabb",)])
                for bi, (t0, n) in enumerate(BLKS):
                    if last and bi == 2:
                        continue
                    pt, pk = proj(ws, wk, 0, KC, lambda kc: hT[:, kc, t0:t0 + n], hkeys(bi), n)
                    nr = n // 64
                    for hh in range(2):
                        pr = slice(hh * 64, (hh + 1) * 64)
                        k.op("dve", lambda: nc.vector.tensor_scalar(
                            out=QT[pr, t0 // 64:t0 // 64 + nr, hh * 64:(hh + 1) * 64],
                            in0=pt[pr, 0:n].rearrange("p (r c) -> p r c", c=64),
                            scalar1=vec("b_in", cfg.Q // 128 + hp)[pr, :], scalar2=0.125, op0=ALU.add, op1=ALU.mult),
                            r=[pk, ("vecs",)], w=[("QT",)])
                for (t0, n, src, skey, bi) in allblk:
                    rk = hkeys(bi) if bi is not None else [("hh", kc) for kc in range(KC)]
                    sfn = (lambda kc: hT[:, kc, t0:t0 + n]) if bi is not None else (lambda kc: h_halo[:, kc, :])
                    pt, pk = proj(ws, wk, 128, KC, sfn, rk, n)
                    k.op("act", lambda: nc.scalar.activation(out=KT[:, t0:t0 + n], in_=pt[:, 0:n], func=AF.Identity,
                                                             bias=vec("b_in", cfg.K // 128 + hp), scale=1.0),
                         r=[pk, ("vecs",)], w=[("KT",)])
                ws, wk = w_next()
                for (t0, n, src, skey, bi) in allblk:
                    rk = hkeys(bi) if bi is not None else [("hh", kc) for kc in range(KC)]
                    sfn = (lambda kc: hT[:, kc, t0:t0 + n]) if bi is not None else (lambda kc: h_halo[:, kc, :])
                    pt, pk = proj(ws, wk, 0, KC, sfn, rk, n)
                    k.op("act", lambda: nc.scalar.activation(out=VT[:, t0:t0 + n], in_=pt[:, 0:n], func=AF.Identity,
                                                             bias=vec("b_in", cfg.V // 128 + hp), scale=1.0),
                         r=[pk, ("vecs",)], w=[("VT",)])
                for t4 in range(0, 14, 4):
                    nt = min(4, 14 - t4)
                    pt, pk = ps_get()
                    ptb = pt.bitcast(BF16)

                    def emit():
                        ins = None
                        for tt in range(nt):
                            ins = nc.tensor.transpose(ptb[:, tt * 128:(tt + 1) * 128],
                                                      VT[:, (t4 + tt) * 128:(t4 + tt + 1) * 128], ident)
                        return ins

                    k.op("pe", emit, r=[("VT",), ("ident",)], w=[pk])
                    k.op("act", lambda: nc.scalar.copy(out=Vt[:, t4:t4 + nt, :],
                                                       in_=ptb[:, 0:nt * 128].rearrange("p (a b) -> p a b", b=128)),
                         r=[pk], w=[("Vt",)])
                def att_row(j, rr, po, ko, pS, kS, iee):
                    if j < 16:
                        loc = chunk_list(j)
                        chunks = [(pair_tok(P), True) for P in loc] + [(1024, False), (1152, False)]
                    else:
                        loc = []
                        chunks = [(1024, False), (1152, False)]
                    nloc = len(loc)
                    nch = len(chunks)
                    st = {}

                    def sreg(ci):
                        return (st["psa"] if ci < 4 else st["psb"])[:, (ci % 4) * 128:(ci % 4 + 1) * 128]

                    eb = ET[iee % 2]
                    ekey = ("ET", iee % 2)

                    def stage_s():
                        st["psa"], st["ka"] = ps_get()
                        st["psb"], st["kb"] = ps_get()

                        def emit():
                            ins = None
                            for ci, (tk, isloc) in enumerate(chunks):
                                ins = nc.tensor.matmul(sreg(ci), lhsT=KT[:, tk:tk + 128], rhs=QT[:, j, :],
                                                       start=True, stop=not isloc)
                                if isloc:
                                    s_ = 2 * loc[ci] - j + 4
                                    nc.tensor.matmul(sreg(ci), lhsT=ident, rhs=tabb[:, :, s_, :],
                                                     start=False, stop=False)
                                    idx = j * 6 + ci
                                    ins = nc.tensor.matmul(sreg(ci), lhsT=selb[0:2, :],
                                                           rhs=rm2b[0:2, idx:idx + 1].broadcast_to([2, 128]),
                                                           start=False, stop=True)
                            return ins

                        k.op("pe", emit, r=[("KT",), ("QT",), ("tabb",), ("ident",), ("selb",), ("rm2b",)],
                             w=[st["ka"], st["kb"]])

                    def stage_e():
                        ka, kb = st["ka"], st["kb"]
                        na = min(nch, 4)
                        k.op("act", lambda: nc.scalar.activation(
                            out=eb[:, 0:na, :], in_=st["psa"][:, 0:na * 128].rearrange("p (a b) -> p a b", b=128),
                            func=AF.Exp), r=[ka], w=[ekey])
                        if nch > 4:
                            k.op("act", lambda: nc.scalar.activation(
                                out=eb[:, 4:nch, :],
                                in_=st["psb"][:, 0:(nch - 4) * 128].rearrange("p (a b) -> p a b", b=128),
                                func=AF.Exp), r=[kb], w=[ekey])

                    def stage_pv():
                        def emit2():
                            ins = None
                            for ci, (tk, _) in enumerate(chunks):
                                nc.tensor.matmul(po[:, rr * 128:(rr + 1) * 128], lhsT=Vt[:, tk // 128, :],
                                                 rhs=eb[:, ci, :], start=(ci == 0), stop=(ci == nch - 1))
                                ins = nc.tensor.matmul(pS[:, rr * 128:(rr + 1) * 128], lhsT=ones1,
                                                       rhs=eb[:, ci, :], start=(ci == 0), stop=(ci == nch - 1))
                            return ins

                        k.op("pe", emit2, r=[ekey, ("Vt",), ("ones1",)], w=[ko, kS])
                        if rr == 3:
                            r4 = j - 3
                            k.op("dve", lambda: nc.vector.reciprocal(out=rs.rearrange("p a b -> p (a b)"),
                                                                     in_=pS[:, :]), r=[kS], w=[("rs",)])
                            tq = r4 * 64
                            bi_q = 0 if tq < 512 else (1 if tq < 1024 else 2)
                            for hh in range(2):
                                pr = slice(hh * 64, (hh + 1) * 64)
                                cs = slice(hh * 64, (hh + 1) * 64)
                                k.op("dve", lambda: nc.vector.tensor_tensor(
                                    out=att[pr, hp, tq:tq + 256].rearrange("p (r c) -> p r c", c=64),
                                    in0=po[pr, :].rearrange("p (r c) -> p r c", c=128)[:, :, cs],
                                    in1=rs[pr, :, cs], op=ALU.mult),
                                    r=[ko, ("rs",)], w=[("att", hp, bi_q)])

                    return stage_s, stage_e, stage_pv

                pending = None
                grp = None
                for j in range(nqrows):
                    rr = j % 4
                    if rr == 0:
                        grp = ps_get("acc") + ps_get("acc")
                    st_s, st_e, st_pv = att_row(j, rr, grp[0], grp[1], grp[2], grp[3], ie)
                    ie += 1
                    st_s()
                    if pending is not None:
                        pending()
                    st_e()
                    pending = st_pv
                pending()

            dump("att", att, li=li)
            k.fence()
            ps_pools(list(range(8)), [])
            R2.reset()
            RX.reset()
            bog_derive(0)
            mod_derive(1)
            bog_derive(1)
            merged = R2.alloc(BF16, [KC, T])
            sgb = [RX.alloc(F32, [512]) for _ in range(2)]
            tbb = [RX.alloc(F32, [512]) for _ in range(2)]
            accb = RX.alloc(F32, [T])
            brin = [(convin, "convin"), (sguin, "sguin"), (att, "att")]
            nblk_b = 2 if last else 3
            BBLK = BLKS[:2] if last else (BLKS[:2] + [(1024, TQ)])

            def gather_q(buf3, nch, rkeys, wkeys):
                dst = buf3[:, 0:nch, 1024:1024 + TQ]
                k.op("dve", lambda: nc.vector.tensor_scalar(out=dst, in0=dst, scalar1=qmask[:, 0:1], scalar2=None,
                                                            op0=ALU.mult), r=rkeys + [("qmask",)], w=wkeys)
                for i in range(1, 4):
                    k.op("dve", lambda: nc.vector.scalar_tensor_tensor(
                        out=dst, in0=buf3[:, 0:nch, 1024 + i * TQ:1024 + (i + 1) * TQ], scalar=qmask[:, i:i + 1],
                        in1=dst, op0=ALU.mult, op1=ALU.add), r=rkeys + [("qmask",)], w=wkeys)

            if not last:
                gather_q(hT, KC, [("h", kc, 2) for kc in range(KC)], [("h", kc, 2) for kc in range(KC)])
                for buf, bkey in [(convin, "convin"), (sguin, "sguin"), (att, "att")]:
                    gather_q(buf, CC, [(bkey, c, 2) for c in range(CC)], [(bkey, c, 2) for c in range(CC)])
            ib = 0
            for f in range(KC):
                if f > 0 and not mod_in:
                    mod_blocks(1, nxt=True)
                for b in range(3):
                    ws, wk = w_next()
                    buf, bkey = brin[b]
                    for bi, (t0, n) in enumerate(BBLK):
                        pg, kg = proj(ws, wk, 0, KC, lambda kc: hT[:, kc, t0:t0 + n], hkeys(bi), n)
                        py, ky = proj(ws, wk, 128, CC, lambda kc: buf[:, kc, t0:t0 + n],
                                      [(bkey, c, bi) for c in range(CC)], n)
                        sb = sgb[ib % 2]
                        k.op("act", lambda: nc.scalar.activation(out=sb[:, 0:n], in_=pg[:, 0:n], func=AF.Sigmoid,
                                                                 bias=vec("b_in", cfg.G // 128 + b * KC + f),
                                                                 scale=1.0),
                             r=[kg, ("vecs",)], w=[("sgb", ib % 2)])
                        if b == 0:
                            k.op("dve", lambda: nc.vector.tensor_tensor(out=accb[:, t0:t0 + n], in0=py[:, 0:n],
                                                                        in1=sb[:, 0:n], op=ALU.mult),
                                 r=[ky, ("sgb", ib % 2)], w=[("accb", bi)])
                        else:
                            tb = tbb[ib % 2]
                            k.op("dve", lambda: nc.vector.tensor_tensor(out=tb[:, 0:n], in0=py[:, 0:n],
                                                                        in1=sb[:, 0:n], op=ALU.mult),
                                 r=[ky, ("sgb", ib % 2)], w=[("tbb", ib % 2)])
                            dst = accb[:, t0:t0 + n] if b == 1 else merged[:, f, t0:t0 + n]
                            wkeys = [("accb", bi)] if b == 1 else [("merged", f, bi)]
                            k.op("dve", lambda: nc.vector.tensor_tensor(out=dst, in0=tb[:, 0:n],
                                                                        in1=accb[:, t0:t0 + n], op=ALU.add),
                                 r=[("tbb", ib % 2), ("accb", bi)], w=wkeys)
                        ib += 1

            if not mod_in:
                mod_blocks(1, nxt=True)
            dump("merged", merged, li=li)
            k.fence()
            for kc in range(KC):
                k.dma("sp", xT[:, kc, :], xsp_d[kc * 128:(kc + 1) * 128, :], r=[("xsp", kc)],
                      w=[xkey(kc, bi) for bi in range(3)])
            if not last:
                gather_q(xT, KC, [xkey(kc, 2) for kc in range(KC)], [xkey(kc, 2) for kc in range(KC)])
            for o0 in range(0, KC, 2):
                ws, wk = w_next()
                for oj in range(2):
                    o = o0 + oj
                    for bi, (t0, n) in enumerate(BBLK):
                        r = 0 if bi < 2 else 1
                        pt, pk = proj(ws, wk, oj * 128, KC, lambda kc: merged[:, kc, t0:t0 + n],
                                      [("merged", c, bi) for c in range(KC)], n)
                        k.op("dve", lambda: nc.vector.scalar_tensor_tensor(
                            out=xT[:, o, t0:t0 + n], in0=pt[:, 0:n], scalar=mod(2, o, r), in1=xT[:, o, t0:t0 + n],
                            op0=ALU.mult, op1=ALU.add), r=[pk, xkey(o, bi), ("mod", 2)], w=[xkey(o, bi)])
                        k.op("act", lambda: nc.scalar.activation(out=xT[:, o, t0:t0 + n], in_=xT[:, o, t0:t0 + n],
                                                                 func=AF.Identity, bias=bog[:, 0, o, r:r + 1],
                                                                 scale=1.0),
                             r=[xkey(o, bi), ("bog", 0)], w=[xkey(o, bi)])

            dump("xmid", xT, li=li)
            R2.reset()
            RX.reset()
            fT = R2.alloc(BF16, [KC, T])
            scratch = ([RX.alloc(BF16, [512]) for _ in range(2)], RX.alloc(F32, [512]),
                       [RX.alloc(F32, [512]) for _ in range(2)])
            for bi, (t0, n) in enumerate(BBLK):
                r = 0 if bi < 2 else 1
                rms_mod(lambda kc: xT[:, kc, t0:t0 + n], lambda kc: [xkey(kc, bi)], n,
                        lambda kc: hT[:, kc, t0:t0 + n], lambda kc: [("h", kc, bi)],
                        lambda kc: Amod[:, 1, kc, r:r + 1], lambda kc: mod(3, kc, r), scratch, [("mod", 3), ("amod", 1)])
            rl = scratch[2]
            ir = 0
            dcnt_c = [0]
            for g in range(FC // KC):
                for j0 in range(0, KC, 2):
                    ws, wk = w_next()
                    for cj in range(2):
                        j = j0 + cj
                        for bi, (t0, n) in enumerate(BBLK):
                            pt, pk = proj(ws, wk, cj * 128, KC, lambda kc: hT[:, kc, t0:t0 + n], hkeys(bi), n)
                            rb = rl[ir % 2]
                            k.op("act", lambda: nc.scalar.activation(out=rb[:, 0:n], in_=pt[:, 0:n], func=AF.Relu,
                                                                     bias=vec("b_ff1", g * KC + j), scale=1.0),
                                 r=[pk, ("vecs",)], w=[("ntmp", ir % 2)])
                            k.op("dve", lambda: nc.vector.tensor_tensor(out=fT[:, j, t0:t0 + n], in0=rb[:, 0:n],
                                                                        in1=rb[:, 0:n], op=ALU.mult),
                                 r=[("ntmp", ir % 2)], w=[("merged", j, bi)])
                            ir += 1
                    dcnt_c[0] += 1
                    if not mod_in and dcnt_c[0] % 2 == 0:
                        mod_blocks(1, nxt=True)
                for o0 in range(0, KC, 2):
                    ws, wk = w_next()
                    for oj in range(2):
                        o = o0 + oj
                        for bi, (t0, n) in enumerate(BBLK):
                            r = 0 if bi < 2 else 1
                            pt, pk = proj(ws, wk, oj * 128, KC, lambda kc: fT[:, kc, t0:t0 + n],
                                          [("merged", c, bi) for c in range(KC)], n)
                            k.op("dve", lambda: nc.vector.scalar_tensor_tensor(
                                out=xT[:, o, t0:t0 + n], in0=pt[:, 0:n], scalar=mod(5, o, r), in1=xT[:, o, t0:t0 + n],
                                op0=ALU.mult, op1=ALU.add), r=[pk, xkey(o, bi), ("mod", 5)], w=[xkey(o, bi)])
                    dcnt_c[0] += 1
                    if not mod_in and dcnt_c[0] % 2 == 0:
                        mod_blocks(1, nxt=True)
            for o in range(KC):
                for bi, (t0, n) in enumerate(BBLK):
                    r = 0 if bi < 2 else 1
                    k.op("act", lambda: nc.scalar.activation(out=xT[:, o, t0:t0 + n], in_=xT[:, o, t0:t0 + n],
                                                             func=AF.Identity, bias=bog[:, 1, o, r:r + 1], scale=1.0),
                         r=[xkey(o, bi), ("bog", 1)], w=[xkey(o, bi)])

        k.fence()
        R2.reset()
        if final_norm:
            sq = [R2.alloc(BF16, [512]) for _ in range(2)]
            rstd = R2.alloc(F32, [512])
            obuf = [R2.alloc(F32, [512]) for _ in range(2)]
            io = 0
            for bi, (t0, n) in enumerate(BLKS[:2]):
                pt, pk = ps_get()
                for kc in range(KC):
                    sqb = sq[kc % 2]
                    k.op("act", lambda: nc.scalar.activation(out=sqb[:, 0:n], in_=xT[:, kc, t0:t0 + n], func=AF.Square),
                         r=[xkey(kc, bi)], w=[("sq", kc % 2)])
                    k.op("pe", lambda: nc.tensor.matmul(pt[:, 0:n], lhsT=onesD, rhs=sqb[:, 0:n],
                                                        start=(kc == 0), stop=(kc == KC - 1)),
                         r=[("sq", kc % 2), ("onesD",)], w=[pk])
                k.op("act", lambda: nc.scalar.activation(out=rstd[:, 0:n], in_=pt[:, 0:n], func=AF.Sqrt, bias=EPS,
                                                         scale=1.0), r=[pk], w=[("rstd",)])
                k.op("dve", lambda: nc.vector.reciprocal(out=rstd[:, 0:n], in_=rstd[:, 0:n]), r=[("rstd",)],
                     w=[("rstd",)])
                for kc in range(KC):
                    ob = obuf[io % 2]
                    k.op("dve", lambda: nc.vector.scalar_tensor_tensor(out=ob[:, 0:n], in0=xT[:, kc, t0:t0 + n],
                                                                       scalar=fgv[:, kc:kc + 1], in1=rstd[:, 0:n],
                                                                       op0=ALU.mult, op1=ALU.mult),
                         r=[xkey(kc, bi), ("rstd",), ("fgv",)], w=[("obuf", io % 2)])
                    k.dma("sp", out_d[kc * 128:(kc + 1) * 128, t0:t0 + n], ob[:, 0:n], r=[("obuf", io % 2)],
                          w=[("out", kc, bi)])
                    io += 1
        else:
            for kc in range(KC):
                k.dma("sp", out_d[kc * 128:(kc + 1) * 128, :], xT[:, kc, :], r=[xkey(kc, bi) for bi in range(3)],
                      w=[("out", kc)])
        if mod_next:
            k.dma("sp", modout_d[:, :], modvn.rearrange("p a b -> p (a b)"), r=[("modn",)], w=[("modout",)])
        sp = k.eng["sp"]
        for i, s in enumerate(k.dsems):
            if k.dsem_val[i] > 0:
                sp.h.wait_ge(s, k.dsem_val[i])
        assert wq_used[0] == len(wq), (wq_used[0], len(wq))
    return nc


def _prep_common(cfg, inp, layers):
    L = len(layers)
    D = cfg.D
    f = lambda a: np.ascontiguousarray(np.asarray(a, np.float32))
    com = {
        "vecs": np.concatenate([pack_vecs(cfg, inp, l) for l in layers], 0),
        "final_g": fm(inp["final_g"]),
        "tab": np.concatenate([build_tab(cfg, inp["na_rpb"][l]) for l in layers], 0),
        "wsT": np.concatenate([f(np.asarray(inp["sgu_w"][l]).transpose(0, 2, 1)).reshape(cfg.CC * 128, 128)
                               for l in layers], 0),
        "bsb": np.concatenate([np.broadcast_to(f(inp["sgu_b"][l]).reshape(1, cfg.CC * 128), (128, cfg.CC * 128))
                               for l in layers], 0).copy(),
    }
    for name, key in [("w_mod", "w_mod"), ("w_in", "w_in"), ("w_conv_out", "w_conv_out"), ("w_sgu_out", "w_sgu_out"),
                      ("w_na_out", "w_na_out"), ("w_o", "w_o"), ("w_ff1", "w_ff1"), ("w_ff2", "w_ff2")]:
        a = np.asarray(inp[key], np.float32)
        sel = a[layers[0]:layers[-1] + 1] if list(layers) == list(range(layers[0], layers[-1] + 1)) else a[list(layers)]
        com[name] = np.ascontiguousarray(sel).reshape(L * a.shape[1], a.shape[2])
    return com


def _core_static(cfg, inp, core):
    b, q = core // 4, core % 4
    cT = np.stack([fm(inp["c"][b]), fm(inp["c_ctx"])], axis=-1).reshape(128, cfg.KC * 2)
    hm = np.zeros((128, 2), np.float32)
    hm[:, 0] = 1.0 if q > 0 else 0.0
    hm[:, 1] = 1.0 if q < 3 else 0.0
    return {"cT": np.ascontiguousarray(cT), "halomask": hm,
            "ident": np.eye(128, dtype=np.float32),
            "qmask": np.ascontiguousarray(np.broadcast_to(np.eye(4, dtype=np.float32)[q][None, :], (128, 4))),
            "rm2": np.ascontiguousarray(build_rowmask(q).reshape(2, 64, 96)[:, 0, :]),
            "sel": np.ascontiguousarray(np.repeat(np.eye(2, dtype=np.float32), 64, axis=1))}


def _halo_from(xl_full, core):
    b, q = core // 4, core % 4
    D = xl_full.shape[-1]
    h = np.zeros((THALO, D), np.float32)
    t0 = q * TOWN
    if q > 0:
        h[0:256] = xl_full[b, t0 - 256:t0]
    if q < 3:
        h[256:512] = xl_full[b, t0 + TOWN:t0 + TOWN + 256]
    return np.ascontiguousarray(h.T)


_PROG_CACHE = {}


def run_unfused(cfg, inp, trace=None, debug=False, dbg_out=None):
    L = cfg.L
    xl = np.asarray(inp["x"], np.float32)
    xc = np.asarray(inp["ctx"], np.float32)
    statics = [_core_static(cfg, inp, c) for c in range(8)]
    out = None
    for l in range(L):
        last = l == L - 1
        mod_in = l > 0
        mod_next = not last
        keyp = (cfg.D, last, mod_in, mod_next)
        if keyp not in _PROG_CACHE:
            _PROG_CACHE[keyp] = build_program(cfg, [0], final_norm=last, debug=(debug and l == 0),
                                              mod_in=mod_in, mod_next=mod_next)
        nc = _PROG_CACHE[keyp]
        com = _prep_common(cfg, inp, [l])
        if mod_in:
            del com["w_mod"]
        if mod_next:
            com["w_mod_next"] = np.ascontiguousarray(np.asarray(inp["w_mod"][l + 1], np.float32))
        in_maps = []
        for c in range(8):
            b, q = c // 4, c % 4
            xo = np.concatenate([xl[b, q * TOWN:(q + 1) * TOWN], xc[b]], 0)
            m = dict(com)
            m.update(statics[c])
            m["xT"] = np.ascontiguousarray(xo.T)
            m["xhT"] = _halo_from(xl, c)
            if mod_in:
                m["modv_in"] = modv_prev[c]
            in_maps.append(m)
        res = run_bass_kernel_spmd(nc, in_maps, core_ids=list(range(8)))
        if mod_next:
            modv_prev = [np.ascontiguousarray(res.results[c]["modv_out"]) for c in range(8)]
        if dbg_out is not None and l == 0:
            for c in range(8):
                dbg_out.append({kk: vv for kk, vv in res.results[c].items() if kk.startswith('dbg_')})
        if last:
            out = np.zeros((BATCH, SEQ, cfg.D), np.float32)
            for c in range(8):
                b, q = c // 4, c % 4
                out[b, q * TOWN:(q + 1) * TOWN] = res.results[c]["outT"].T
        else:
            xl_n = np.zeros_like(xl)
            xc_n = np.zeros_like(xc)
            for c in range(8):
                b, q = c // 4, c % 4
                o = res.results[c]["outT"].T
                xl_n[b, q * TOWN:(q + 1) * TOWN] = o[:TOWN]
                xc_n[b, q * TQ:(q + 1) * TQ] = o[TOWN:TOWN + TQ]
            xl, xc = xl_n, xc_n
            if trace is not None:
                trace.append((xl, xc))
    return out


def kernel(**inputs):
    cfg = Cfg(2048, 4)
    return run_unfused(cfg, inputs)
```

```python
import numpy as np
from contextlib import ExitStack
import concourse.bass as bass
import concourse.mybir as mybir
from concourse.bass_utils import run_bass_kernel_spmd

F32 = mybir.dt.float32
BF16 = mybir.dt.bfloat16
I32 = mybir.dt.int32
AF = mybir.ActivationFunctionType
ALU = mybir.AluOpType

GRID_W = 64
CTX_LEN = 256
SEQ = 4096
BATCH = 2
EPS = 1e-6
NEG = -1e30
CONV_WIDTH = 31
NA_KH, NA_KW = 8, 16
TOWN, TCTX, T, THALO, TALL = 1024, 256, 1280, 512, 1792
TQ = 64
BLKS = [(0, 512), (512, 512), (1024, 256)]
NCOL = 256
NSLOT = 3


class Cfg:
    def __init__(self, D, L):
        self.D = D
        self.L = L
        self.KC = D // 128
        self.CD = D // 2
        self.CC = self.CD // 128
        self.NH = self.CD // 64
        self.HP = self.NH // 2
        self.DFF = 4 * D
        self.FC = self.DFF // 128
        self.IN = 2 * self.CD + 2 * self.CD + 3 * self.CD + 3 * D
        self.A1, self.A2 = 0, self.CD
        self.SU, self.SV = 2 * self.CD, 3 * self.CD
        self.Q, self.K, self.V = 4 * self.CD, 5 * self.CD, 6 * self.CD
        self.G = 7 * self.CD
        o = 0
        self.voff = {}
        for name, n in [("n1g", self.KC), ("n2g", self.KC), ("b_in", self.IN // 128), ("conv_w", self.CC * 31),
                        ("conv_b", self.CC), ("cln_g", self.CC), ("cln_b", self.CC), ("sln_g", self.CC),
                        ("sln_b", self.CC), ("b_o", self.KC), ("b_ff1", self.FC), ("b_ff2", self.KC),
                        ("b_mod", 6 * self.KC), ("b_mod_next", 6 * self.KC)]:
            self.voff[name] = (o, n)
            o += n
        self.NV = o


def fm(v):
    v = np.asarray(v, np.float32)
    return np.ascontiguousarray(v.reshape(-1, 128).T)


def pack_vecs(cfg, inp, l):
    out = np.zeros((128, cfg.NV), np.float32)

    def put(name, arr):
        o, n = cfg.voff[name]
        assert arr.shape == (128, n), (name, arr.shape, n)
        out[:, o:o + n] = arr

    put("n1g", fm(inp["norm1_g"][l]))
    put("n2g", fm(inp["norm2_g"][l]))
    put("b_in", fm(inp["b_in"][l]))
    cw = np.asarray(inp["conv_w"][l], np.float32)
    put("conv_w", np.ascontiguousarray(cw.T.reshape(cfg.CC, 128, 31).transpose(1, 0, 2).reshape(128, cfg.CC * 31)))
    put("conv_b", fm(inp["conv_b"][l]))
    put("cln_g", fm(inp["conv_ln_g"][l]))
    put("cln_b", fm(inp["conv_ln_b"][l]))
    put("sln_g", fm(inp["sgu_ln_g"][l]))
    put("sln_b", fm(inp["sgu_ln_b"][l]))
    put("b_o", fm(inp["b_o"][l]))
    put("b_ff1", fm(inp["b_ff1"][l]))
    put("b_ff2", fm(inp["b_ff2"][l]))
    put("b_mod", fm(inp["b_mod"][l]))
    if l + 1 < np.asarray(inp["b_mod"]).shape[0]:
        put("b_mod_next", fm(inp["b_mod"][l + 1]))
    return out


def build_tab(cfg, rpb):
    rpb = np.asarray(rpb, np.float32)
    c = np.arange(64)
    w = np.arange(64)
    ws = np.clip(c - NA_KW // 2, 0, GRID_W - NA_KW)
    colvalid = (w[:, None] >= ws[None, :]) & (w[:, None] < ws[None, :] + NA_KW)
    dc = np.clip(w[:, None] - c[None, :] + NA_KW - 1, 0, 2 * NA_KW - 2)
    tab = np.full((2, 64, cfg.NH, 16, 64), NEG, np.float32)
    for i in range(2):
        for s in range(16):
            dr = s + i - 8
            if -7 <= dr <= 7:
                vals = rpb[:, dr + NA_KH - 1][:, dc]
                vals = np.where(colvalid[None], vals, np.float32(NEG))
                tab[i, :, :, s, :] = vals.transpose(1, 0, 2)
    return np.ascontiguousarray(tab.reshape(128, cfg.NH * 16 * 64))


def chunk_list(j):
    lo, hi = j // 2, (j + 7) // 2
    if j <= 1:
        lo, hi = 0, 5
    elif j <= 3:
        lo, hi = 1, 5
    elif j == 14:
        lo, hi = 6, 10
    elif j == 15:
        lo, hi = 6, 11
    return list(range(lo, hi + 1))


def build_rowmask(q):
    rm = np.full((2, 64, 16, 6), NEG, np.float32)
    rows = SEQ // GRID_W
    for j in range(16):
        r = 16 * q + j
        w0 = min(max(r - NA_KH // 2, 0), rows - NA_KH)
        for ci, P in enumerate(chunk_list(j)):
            for i in range(2):
                kr = 16 * q - 4 + 2 * P + i
                if 0 <= kr < rows and w0 <= kr < w0 + NA_KH:
                    rm[i, :, j, ci] = 0.0
    return np.ascontiguousarray(rm.reshape(128, 16 * 6))


def pair_tok(P):
    if P < 2:
        return 1280 + P * 128
    if P < 10:
        return (P - 2) * 128
    return 1536 + (P - 10) * 128


class EngState:
    def __init__(self, key, h, sem):
        self.key, self.h, self.sem = key, h, sem
        self.count = 0
        self.waited = {}


class BufState:
    __slots__ = ("lw", "rd")

    def __init__(self):
        self.lw = None
        self.rd = {}


class K:
    def __init__(self, nc, es, cfg):
        self.nc, self.es, self.cfg = nc, es, cfg
        self.eng = {}
        for key, h in [("pe", nc.tensor), ("act", nc.scalar), ("dve", nc.vector), ("pool", nc.gpsimd),
                       ("sp", nc.sync)]:
            self.eng[key] = EngState(key, h, es.enter_context(nc.semaphore("s_" + key)))
        self.bufs = {}
        self.dsems = [es.enter_context(nc.semaphore("s_d%d" % i)) for i in range(8)]
        self.dsem_val = [0] * 8
        self.dsem_rr = 0
        self.wsem = [es.enter_context(nc.semaphore("s_w%d" % i)) for i in range(NSLOT)]
        self.wsem_val = [0] * NSLOT

    def _deps(self, r, w):
        deps = {}

        def add(tok):
            if tok is None:
                return
            key, sem, val = tok
            if key not in deps or deps[key][1] < val:
                deps[key] = (sem, val)

        for k in r:
            st = self.bufs.get(k)
            if st is not None:
                add(st.lw)
        for k in w:
            st = self.bufs.get(k)
            if st is not None:
                add(st.lw)
                for d in st.rd.values():
                    add(d)
        return deps

    def _wait(self, E, deps, skip_self):
        for key, (sem, val) in deps.items():
            if skip_self and key == E.key:
                continue
            if E.waited.get(key, 0) < val:
                E.h.wait_ge(sem, val)
                E.waited[key] = val

    def _record(self, tok, r, w):
        for k in w:
            st = self.bufs.get(k)
            if st is None:
                st = self.bufs[k] = BufState()
            st.lw = tok
            st.rd = {}
        for k in r:
            st = self.bufs.get(k)
            if st is None:
                st = self.bufs[k] = BufState()
            st.rd[tok[0]] = tok

    def op(self, e, fn, r=(), w=()):
        E = self.eng[e]
        self._wait(E, self._deps(r, w), skip_self=(e == "pe"))
        ins = fn()
        E.count += 1
        ins.then_inc(E.sem, 1)
        self._record((E.key, E.sem, E.count), r, w)

    def dma(self, q, out, in_, r=(), w=(), wslot=None):
        E = self.eng[q]
        self._wait(E, self._deps(r, w), skip_self=True)
        if wslot is not None:
            sem = self.wsem[wslot]
            self.wsem_val[wslot] += 16
            val = self.wsem_val[wslot]
            key = "w%d" % wslot
        else:
            i = self.dsem_rr
            self.dsem_rr = (i + 1) % len(self.dsems)
            sem = self.dsems[i]
            key = "d%d" % i
            if self.dsem_val[i] > 0 and E.waited.get(key, 0) < self.dsem_val[i]:
                E.h.wait_ge(sem, self.dsem_val[i])
                E.waited[key] = self.dsem_val[i]
            self.dsem_val[i] += 16
            val = self.dsem_val[i]
        E.h.dma_start(out=out, in_=in_).then_inc(sem, 16)
        self._record((key, sem, val), r, w)

    def fence(self):
        names = ["pe", "act", "dve", "sp"]
        for a in names:
            A = self.eng[a]
            for b in ["pe", "act", "dve"]:
                if a == b:
                    continue
                B = self.eng[b]
                if B.count > 0 and A.waited.get(b, 0) < B.count:
                    A.h.wait_ge(B.sem, B.count)
                    A.waited[b] = B.count
            if a != "sp":
                for i, s in enumerate(self.dsems):
                    key = "d%d" % i
                    if self.dsem_val[i] > 0 and A.waited.get(key, 0) < self.dsem_val[i]:
                        A.h.wait_ge(s, self.dsem_val[i])
                        A.waited[key] = self.dsem_val[i]


class Region:
    def __init__(self, t, nbytes):
        self.t = t
        self.f32 = t
        self.b16 = t.bitcast(BF16)
        self.nbytes = nbytes
        self.off = 0

    def reset(self, off=0):
        self.off = off

    def alloc(self, dtype, shape):
        n = int(np.prod(shape))
        esz = 4 if dtype == F32 else 2
        self.off = (self.off + 31) // 32 * 32
        o = self.off
        assert o + n * esz <= self.nbytes, ("region overflow", o, n * esz, self.nbytes)
        self.off = o + n * esz
        base = self.f32 if dtype == F32 else self.b16
        ap = base[:, o // esz:o // esz + n]
        if len(shape) == 2:
            ap = ap.rearrange("p (a b) -> p a b", a=shape[0])
        elif len(shape) == 3:
            ap = ap.rearrange("p (a b c) -> p a b c", a=shape[0], b=shape[1])
        return ap


def build_program(cfg, layers, final_norm, first_from_input=True, debug=False, mod_in=False, mod_next=False):
    nc = bass.Bass("TRN2", target_bir_lowering=False)
    D, KC, CD, CC, HP, FC = cfg.D, cfg.KC, cfg.CD, cfg.CC, cfg.HP, cfg.FC
    NL = len(layers)

    def din(name, shape, dt=F32):
        return nc.dram_tensor(name, list(shape), dt, kind="ExternalInput").ap()

    xT_d = din("xT", [D, T])
    xhT_d = din("xhT", [D, THALO])
    cT_d = din("cT", [128, KC * 2])
    vecs_d = din("vecs", [NL * 128, cfg.NV])
    fg_d = din("final_g", [128, KC])
    ident_d = din("ident", [128, 128])
    rm2_d = din("rm2", [2, 96])
    qmask_d = din("qmask", [128, 4])
    sel_d = din("sel", [2, 128])
    halomask_d = din("halomask", [128, 2])
    tab_d = din("tab", [NL * 128, cfg.NH * 1024])
    wsT_d = din("wsT", [NL * CC * 128, 128])
    bsb_d = din("bsb", [NL * 128, CC * 128])
    w_mod_d = din("w_mod", [NL * D, 6 * D]) if not mod_in else None
    w_modn_d = din("w_mod_next", [D, 6 * D]) if mod_next else None
    modin_d = din("modv_in", [128, 12 * KC]) if mod_in else None
    modout_d = nc.dram_tensor("modv_out", [128, 12 * KC], F32, kind="ExternalOutput").ap() if mod_next else None
    w_in_d = din("w_in", [NL * D, cfg.IN])
    w_co_d = din("w_conv_out", [NL * CD, D])
    w_so_d = din("w_sgu_out", [NL * CD, D])
    w_no_d = din("w_na_out", [NL * CD, D])
    w_o_d = din("w_o", [NL * D, D])
    w_f1_d = din("w_ff1", [NL * D, cfg.DFF])
    w_f2_d = din("w_ff2", [NL * cfg.DFF, D])
    if final_norm:
        out_d = nc.dram_tensor("outT", [D, TOWN], F32, kind="ExternalOutput").ap()
    else:
        out_d = nc.dram_tensor("outT", [D, T], F32, kind="ExternalOutput").ap()
    xsp_d = nc.dram_tensor("xspill", [D, T], F32, kind="Internal").ap()
    dbg = {}

    with ExitStack() as es:
        k = K(nc, es, cfg)
        R1B = KC * T * 4
        R2B = max(KC * T * 2, 40960)
        XB = 13 * 1024
        hT_t = es.enter_context(nc.sbuf_tensor("hT", [128, KC * T], BF16))
        R1 = Region(es.enter_context(nc.sbuf_tensor("R1", [128, R1B // 4], F32)), R1B)
        R2 = Region(es.enter_context(nc.sbuf_tensor("R2", [128, R2B // 4], F32)), R2B)
        RX = Region(es.enter_context(nc.sbuf_tensor("RX", [128, XB // 4], F32)), XB)
        WS_t = es.enter_context(nc.sbuf_tensor("WS", [128, NSLOT * KC * NCOL], BF16))
        NM = cfg.NV + 34 * KC + 1100
        misc = Region(es.enter_context(nc.sbuf_tensor("misc", [128, NM + 256], F32)), (NM + 256) * 4)
        psum = [es.enter_context(nc.psum_tensor("ps%d" % i, [128, 512], F32)) for i in range(8)]
        ps_state = {"gen": list(range(8)), "rr": 0, "acc": [], "arr": 0}

        def ps_pools(gen, acc):
            ps_state["gen"], ps_state["acc"], ps_state["rr"], ps_state["arr"] = gen, acc, 0, 0

        def ps_get(kind="gen"):
            if kind == "gen":
                lst = ps_state["gen"]
                i = lst[ps_state["rr"] % len(lst)]
                ps_state["rr"] += 1
            else:
                lst = ps_state["acc"]
                i = lst[ps_state["arr"] % len(lst)]
                ps_state["arr"] += 1
            return psum[i], ("ps", i)

        def dump(name, ap, li_only=0, li=0):
            if not debug or li != li_only:
                return
            k.fence()
            shp = [128, int(np.prod(ap.shape[1:]))]
            dt = ap.dtype
            d = nc.dram_tensor("dbg_" + name, shp, dt, kind="ExternalOutput").ap()
            flat = ap
            if len(ap.shape) == 3:
                flat = ap.rearrange("p a b -> p (a b)")
            elif len(ap.shape) == 4:
                flat = ap.rearrange("p a b c -> p (a b c)")
            k.dma("sp", d[:, :], flat, w=[("dbg", name)])
            dbg[name] = shp

        hT = hT_t[:, :].rearrange("p (c t) -> p c t", c=KC)
        WS = WS_t[:, :].rearrange("p (s c n) -> p s c n", s=NSLOT, c=KC)

        vecs = misc.alloc(F32, [cfg.NV])
        sc = misc.alloc(F32, [KC, 2])
        modv = misc.alloc(F32, [6 * KC, 2])
        modvn = misc.alloc(F32, [6 * KC, 2])
        Amod = misc.alloc(F32, [2, KC, 2])
        bog = misc.alloc(F32, [2, KC, 2])
        fgv = misc.alloc(F32, [KC])
        halomask = misc.alloc(F32, [2])
        onesD = misc.alloc(BF16, [128])
        ident = misc.alloc(BF16, [128])
        onesC = misc.alloc(BF16, [128])
        ones1 = misc.alloc(BF16, [128])
        identf = misc.alloc(F32, [128])
        scb = misc.alloc(BF16, [KC, 2])
        rm2f = misc.alloc(F32, [96])
        qmask = misc.alloc(F32, [4])
        rm2b = misc.alloc(BF16, [96])
        self_f = misc.alloc(F32, [128])
        selb = misc.alloc(BF16, [128])
        modrow = misc.alloc(F32, [256])

        def vec(name, i=None, n=1):
            o, cnt = cfg.voff[name]
            if i is None:
                return vecs[:, o:o + cnt]
            return vecs[:, o + i:o + i + n]

        def mod(sec, kc, r):
            return modv[:, sec * KC + kc, r:r + 1]

        k.op("dve", lambda: nc.vector.memset(onesD, 1.0 / D), w=[("onesD",)])
        k.op("dve", lambda: nc.vector.memset(onesC, 1.0 / CD), w=[("onesC",)])
        k.op("dve", lambda: nc.vector.memset(ones1, 1.0), w=[("ones1",)])
        k.dma("sp", identf, ident_d[:, :], w=[("identf",)])
        k.op("dve", lambda: nc.vector.tensor_copy(out=ident, in_=identf), r=[("identf",)], w=[("ident",)])
        k.dma("sp", sc.rearrange("p a b -> p (a b)"), cT_d[:, :], w=[("sc",)])
        k.dma("sp", fgv, fg_d[:, :], w=[("fgv",)])
        k.dma("sp", halomask, halomask_d[:, :], w=[("halomask",)])
        k.dma("sp", rm2f[0:2, :], rm2_d[:, :], w=[("rm2f",)])
        k.dma("sp", qmask, qmask_d[:, :], w=[("qmask",)])
        k.dma("sp", self_f[0:2, :], sel_d[:, :], w=[("self",)])
        k.op("dve", lambda: nc.vector.tensor_copy(out=rm2b[0:2, :], in_=rm2f[0:2, :]), r=[("rm2f",)], w=[("rm2b",)])
        k.op("dve", lambda: nc.vector.tensor_copy(out=selb[0:2, :], in_=self_f[0:2, :]), r=[("self",)], w=[("selb",)])
        k.op("act", lambda: nc.scalar.activation(out=sc, in_=sc, func=AF.Silu), r=[("sc",)], w=[("sc",)])
        k.op("dve", lambda: nc.vector.tensor_copy(out=scb, in_=sc), r=[("sc",)], w=[("scb",)])

        wq = []
        wq_issued = [0]
        wq_used = [0]

        def w_issue(i):
            if i >= len(wq) or i < wq_issued[0]:
                return
            assert i == wq_issued[0]
            s_ = i % NSLOT
            for (src, kc0, c0) in wq[i]:
                rows, cols = src.shape
                nk = rows // 128
                k.dma("pool", WS[:, s_, kc0:kc0 + nk, c0:c0 + cols],
                      src.rearrange("(c p) n -> p c n", p=128), w=[("ws", s_)], wslot=s_)
            wq_issued[0] += 1

        def w_next():
            i = wq_used[0]
            wq_used[0] += 1
            for jj in range(wq_issued[0], i + NSLOT):
                w_issue(jj)
            s_ = i % NSLOT
            return WS[:, s_], ("ws", s_)

        def declare_stream():
            for li, l in enumerate(layers):
                r0 = li * D
                rc = li * CD
                mpos = [0]

                def modq(nb):
                    for _ in range(nb):
                        if mpos[0] >= 6 * KC:
                            return
                        c0 = mpos[0] * 128
                        wq.append([(w_mod_d[r0:r0 + D, c0:c0 + 256], 0, 0)])
                        mpos[0] += 2

                if mod_in:
                    mpos[0] = 6 * KC
                mnpos = [0 if mod_next else 6 * KC]

                def modnq(nb):
                    for _ in range(nb):
                        if mnpos[0] >= 6 * KC:
                            return
                        c0 = mnpos[0] * 128
                        wq.append([(w_modn_d[0:D, c0:c0 + 256], 0, 0)])
                        mnpos[0] += 2

                modq(KC)
                for j in range(0, CC, 2):
                    wq.append([(w_in_d[r0:r0 + D, cfg.SV + j * 128: cfg.SV + (j + 2) * 128], 0, 0)])
                    modq(2)
                    if mod_in:
                        modnq(3)
                for j in range(0, CC, 2):
                    wq.append([(w_in_d[r0:r0 + D, cfg.SU + j * 128: cfg.SU + (j + 2) * 128], 0, 0)])
                    modq(2)
                    if mod_in:
                        modnq(3)
                for j in range(CC):
                    wq.append([(w_in_d[r0:r0 + D, cfg.A1 + j * 128: cfg.A1 + (j + 1) * 128], 0, 0),
                               (w_in_d[r0:r0 + D, cfg.A2 + j * 128: cfg.A2 + (j + 1) * 128], 0, 128)])
                    modq(2)
                    if mod_in:
                        modnq(3)
                assert mpos[0] == 6 * KC
                for j in range(HP):
                    wq.append([(w_in_d[r0:r0 + D, cfg.Q + j * 128: cfg.Q + (j + 1) * 128], 0, 0),
                               (w_in_d[r0:r0 + D, cfg.K + j * 128: cfg.K + (j + 1) * 128], 0, 128)])
                    wq.append([(w_in_d[r0:r0 + D, cfg.V + j * 128: cfg.V + (j + 1) * 128], 0, 0)])
                bo = [w_co_d, w_so_d, w_no_d]
                for f in range(KC):
                    for b in range(3):
                        wq.append([(w_in_d[r0:r0 + D, cfg.G + b * D + f * 128: cfg.G + b * D + (f + 1) * 128], 0, 0),
                                   (bo[b][rc:rc + CD, f * 128:(f + 1) * 128], 0, 128)])
                    if not mod_in:
                        modnq(1)
                for o in range(0, KC, 2):
                    wq.append([(w_o_d[r0:r0 + D, o * 128:(o + 2) * 128], 0, 0)])
                dcnt = 0
                for g in range(FC // KC):
                    for j in range(0, KC, 2):
                        c0 = (g * KC + j) * 128
                        wq.append([(w_f1_d[r0:r0 + D, c0:c0 + 256], 0, 0)])
                        dcnt += 1
                        if not mod_in and dcnt % 2 == 0:
                            modnq(1)
                    for o in range(0, KC, 2):
                        rr = li * cfg.DFF + g * D
                        wq.append([(w_f2_d[rr:rr + D, o * 128:(o + 2) * 128], 0, 0)])
                        dcnt += 1
                        if not mod_in and dcnt % 2 == 0:
                            modnq(1)
                assert mnpos[0] == 6 * KC, mnpos

        declare_stream()

        def proj(ws, wkey, col0, nkc, rhs_fn, rkeys, n):
            pt, pk = ps_get()

            def emit():
                ins = None
                for kc in range(nkc):
                    ins = nc.tensor.matmul(pt[:, 0:n], lhsT=ws[:, kc, col0:col0 + 128], rhs=rhs_fn(kc),
                                           start=(kc == 0), stop=(kc == nkc - 1))
                return ins

            k.op("pe", emit, r=[wkey] + list(rkeys), w=[pk])
            return pt, pk

        def rms_mod(x_fn, xkeys, n, out_fn, okeys_fn, A_fn, S_fn, scratch, mkeys):
            sq, rstd, tmp = scratch
            pt, pk = ps_get()
            for kc in range(KC):
                sqb = sq[kc % 2]
                k.op("act", lambda: nc.scalar.activation(out=sqb[:, 0:n], in_=x_fn(kc), func=AF.Square),
                     r=xkeys(kc), w=[("sq", kc % 2)])
                k.op("pe", lambda: nc.tensor.matmul(pt[:, 0:n], lhsT=onesD, rhs=sqb[:, 0:n],
                                                    start=(kc == 0), stop=(kc == KC - 1)),
                     r=[("sq", kc % 2), ("onesD",)], w=[pk])
            k.op("act", lambda: nc.scalar.activation(out=rstd[:, 0:n], in_=pt[:, 0:n], func=AF.Sqrt, bias=EPS,
                                                     scale=1.0), r=[pk], w=[("rstd",)])
            k.op("dve", lambda: nc.vector.reciprocal(out=rstd[:, 0:n], in_=rstd[:, 0:n]), r=[("rstd",)],
                 w=[("rstd",)])
            for kc in range(KC):
                tb = tmp[kc % 2]
                k.op("dve", lambda: nc.vector.scalar_tensor_tensor(out=tb[:, 0:n], in0=x_fn(kc), scalar=A_fn(kc),
                                                                   in1=rstd[:, 0:n], op0=ALU.mult, op1=ALU.mult),
                     r=xkeys(kc) + [("rstd",)] + mkeys, w=[("ntmp", kc % 2)])
                k.op("act", lambda: nc.scalar.activation(out=out_fn(kc), in_=tb[:, 0:n], func=AF.Identity,
                                                         bias=S_fn(kc), scale=1.0),
                     r=[("ntmp", kc % 2)] + mkeys, w=okeys_fn(kc))

        def ln_fm(buf, bkey, nch, gname, bname, func, scratch_region, blks):
            mean = scratch_region.alloc(F32, [T])
            rstd = scratch_region.alloc(F32, [T])
            sqs = [scratch_region.alloc(BF16, [512]) for _ in range(2)]
            msq = scratch_region.alloc(F32, [512])
            for bi, (t0, n) in enumerate(blks):
                p1, k1 = ps_get()
                p2, k2 = ps_get()
                for c in range(nch):
                    k.op("pe", lambda: nc.tensor.matmul(p1[:, 0:n], lhsT=onesC, rhs=buf[:, c, t0:t0 + n],
                                                        start=(c == 0), stop=(c == nch - 1)),
                         r=[(bkey, c, bi), ("onesC",)], w=[k1])
                for c in range(nch):
                    sb = sqs[c % 2]
                    k.op("dve", lambda: nc.vector.tensor_tensor(out=sb[:, 0:n], in0=buf[:, c, t0:t0 + n],
                                                                in1=buf[:, c, t0:t0 + n], op=ALU.mult),
                         r=[(bkey, c, bi)], w=[("lnsq", c % 2)])
                    k.op("pe", lambda: nc.tensor.matmul(p2[:, 0:n], lhsT=onesC, rhs=sb[:, 0:n],
                                                        start=(c == 0), stop=(c == nch - 1)),
                         r=[("lnsq", c % 2), ("onesC",)], w=[k2])
                k.op("act", lambda: nc.scalar.copy(out=mean[:, t0:t0 + n], in_=p1[:, 0:n]), r=[k1],
                     w=[("lnmean", bi)])
                k.op("dve", lambda: nc.vector.tensor_tensor(out=msq[:, 0:n], in0=mean[:, t0:t0 + n],
                                                            in1=mean[:, t0:t0 + n], op=ALU.mult),
                     r=[("lnmean", bi)], w=[("lnmsq",)])
                k.op("dve", lambda: nc.vector.tensor_tensor(out=msq[:, 0:n], in0=p2[:, 0:n], in1=msq[:, 0:n],
                                                            op=ALU.subtract),
                     r=[k2, ("lnmsq",)], w=[("lnmsq",)])
                k.op("act", lambda: nc.scalar.activation(out=rstd[:, t0:t0 + n], in_=msq[:, 0:n], func=AF.Sqrt,
                                                         bias=EPS, scale=1.0), r=[("lnmsq",)], w=[("lnrstd", bi)])
                k.op("dve", lambda: nc.vector.reciprocal(out=rstd[:, t0:t0 + n], in_=rstd[:, t0:t0 + n]),
                     r=[("lnrstd", bi)], w=[("lnrstd", bi)])
            tmps = [scratch_region.alloc(F32, [512]) for _ in range(2)]
            i = 0
            for c in range(nch):
                for bi, (t0, n) in enumerate(blks):
                    tb = tmps[i % 2]
                    k.op("dve", lambda: nc.vector.tensor_tensor(out=tb[:, 0:n], in0=buf[:, c, t0:t0 + n],
                                                                in1=mean[:, t0:t0 + n], op=ALU.subtract),
                         r=[(bkey, c, bi), ("lnmean", bi)], w=[("lntmp", i % 2)])
                    k.op("dve", lambda: nc.vector.tensor_tensor(out=tb[:, 0:n], in0=tb[:, 0:n],
                                                                in1=rstd[:, t0:t0 + n], op=ALU.mult),
                         r=[("lntmp", i % 2), ("lnrstd", bi)], w=[("lntmp", i % 2)])
                    k.op("act", lambda: nc.scalar.activation(out=buf[:, c, t0:t0 + n], in_=tb[:, 0:n], func=func,
                                                             bias=vec(bname, c), scale=vec(gname, c)),
                         r=[("lntmp", i % 2), ("vecs",)], w=[(bkey, c, bi)])
                    i += 1

        xT = R1.f32[:, 0:KC * T].rearrange("p (c t) -> p c t", c=KC)
        xkey = lambda kc, bi: ("x", kc, bi)
        for kc in range(KC):
            k.dma("sp", xT[:, kc, :], xT_d[kc * 128:(kc + 1) * 128, :], w=[xkey(kc, bi) for bi in range(3)])

        for li, l in enumerate(layers):
            last = final_norm and (li == NL - 1)
            r0 = li * D
            k.fence()
            R2.reset()
            RX.reset()
            k.dma("sp", vecs, vecs_d[li * 128:(li + 1) * 128, :], w=[("vecs",)])
            modpos = [6 * KC if mod_in else 0]
            modnpos = [0 if mod_next else 6 * KC]

            def mod_blocks(nb, nxt=False):
                pos = modnpos if nxt else modpos
                tgt = modvn if nxt else modv
                bname = "b_mod_next" if nxt else "b_mod"
                for _ in range(nb):
                    if pos[0] >= 6 * KC:
                        return
                    ws, wk = w_next()
                    j0 = pos[0]
                    pos[0] += 2
                    pt, pk = ps_get()

                    def emit():
                        ins = None
                        for kc in range(KC):
                            ins = nc.tensor.matmul(pt[0:2, 0:256], lhsT=scb[:, kc, :], rhs=ws[:, kc, 0:256],
                                                   start=(kc == 0), stop=(kc == KC - 1))
                        return ins

                    k.op("pe", emit, r=[wk, ("scb",)], w=[pk])
                    k.op("act", lambda: nc.scalar.copy(out=modrow[0:2, :], in_=pt[0:2, 0:256]), r=[pk],
                         w=[("modrow",)])
                    pt2, pk2 = ps_get()

                    def emit2():
                        ins = None
                        for cj in range(2):
                            ins = nc.tensor.matmul(pt2[:, 2 * cj:2 * cj + 2], lhsT=modrow[0:2, cj * 128:(cj + 1) * 128],
                                                   rhs=identf[0:2, 0:2], start=True, stop=True)
                        return ins

                    k.op("pe", emit2, r=[("modrow",), ("identf",)], w=[pk2])
                    sec = j0 // KC
                    k.op("dve", lambda: nc.vector.tensor_tensor(
                        out=tgt[:, j0:j0 + 2, :], in0=pt2[:, 0:4].rearrange("p (j r) -> p j r", r=2),
                        in1=vec(bname)[:, j0:j0 + 2].rearrange("p (j o) -> p j o", o=1).broadcast_to([128, 2, 2]),
                        op=ALU.add), r=[pk2, ("vecs",)], w=[("modn",) if nxt else ("mod", sec)])

            def mod_derive(wn):
                gname, sec = [("n1g", 1), ("n2g", 4)][wn]
                for r in range(2):
                    k.op("dve", lambda: nc.vector.scalar_tensor_tensor(
                        out=Amod[:, wn, :, r], in0=modv[:, sec * KC:(sec + 1) * KC, r], scalar=1.0, in1=vec(gname),
                        op0=ALU.add, op1=ALU.mult), r=[("mod", sec), ("vecs",)], w=[("amod", wn)])

            def bog_derive(wn):
                bname, sec = [("b_o", 2), ("b_ff2", 5)][wn]
                for r in range(2):
                    k.op("dve", lambda: nc.vector.tensor_tensor(out=bog[:, wn, :, r],
                                                                in0=modv[:, sec * KC:(sec + 1) * KC, r],
                                                                in1=vec(bname), op=ALU.mult),
                         r=[("mod", sec), ("vecs",)], w=[("bog", wn)])

            if mod_in:
                k.dma("sp", modv.rearrange("p a b -> p (a b)"), modin_d[:, :], w=[("mod", sec_) for sec_ in range(6)])
            mod_blocks(KC)
            mod_derive(0)

            dump("modv", modv, li=li)
            dump("amod", Amod, li=li)
            k.fence()
            R2.reset()
            h_halo = R2.alloc(BF16, [KC, THALO])
            xh_st = R2.alloc(F32, [KC, 256])
            RX.reset()
            scratch = ([RX.alloc(BF16, [512]) for _ in range(2)], RX.alloc(F32, [512]),
                       [RX.alloc(F32, [512]) for _ in range(2)])
            for bi, (t0, n) in enumerate(BLKS):
                r = 0 if bi < 2 else 1
                rms_mod(lambda kc: xT[:, kc, t0:t0 + n], lambda kc: [xkey(kc, bi)], n,
                        lambda kc: hT[:, kc, t0:t0 + n], lambda kc: [("h", kc, bi)],
                        lambda kc: Amod[:, 0, kc, r:r + 1], lambda kc: mod(0, kc, r), scratch, [("mod", 0), ("amod", 0)])
            for hh in range(2):
                src = xhT_d
                k.dma("sp", xh_st, src[:, hh * 256:(hh + 1) * 256].rearrange("(c p) t -> p c t", p=128),
                      w=[("xh",)])
                rms_mod(lambda kc: xh_st[:, kc, :], lambda kc: [("xh",)], 256,
                        lambda kc: h_halo[:, kc, hh * 256:(hh + 1) * 256], lambda kc: [("hh", kc)],
                        lambda kc: Amod[:, 0, kc, 0:1], lambda kc: mod(0, kc, 0), scratch, [("mod", 0), ("amod", 0)])
            for kc in range(KC):
                k.dma("sp", xsp_d[kc * 128:(kc + 1) * 128, :], xT[:, kc, :], r=[xkey(kc, bi) for bi in range(3)],
                      w=[("xsp", kc)])

            dump("tmp0", scratch[2][0], li=li)
            dump("tmp1", scratch[2][1], li=li)
            dump("rstd", scratch[1], li=li)
            dump("xhst", xh_st, li=li)
            dump("h1", hT, li=li)
            dump("hhalo", h_halo, li=li)
            R2.reset(KC * THALO * 2)
            RX.reset()
            R1b = R1.b16
            convin = R1b[:, 0:CC * T].rearrange("p (c t) -> p c t", c=CC)
            sguin = R1b[:, CC * T:2 * CC * T].rearrange("p (c t) -> p c t", c=CC)
            att = R1b[:, 2 * CC * T:3 * CC * T].rearrange("p (c t) -> p c t", c=CC)
            vact = R1b[:, 3 * CC * T:4 * CC * T].rearrange("p (c t) -> p c t", c=CC)
            hkeys = lambda bi: [("h", kc, bi) for kc in range(KC)]
            ABLK = BLKS[:2] if last else BLKS

            for j0 in range(0, CC, 2):
                ws, wk = w_next()
                for cj in range(2):
                    j = j0 + cj
                    for bi, (t0, n) in enumerate(ABLK):
                        pt, pk = proj(ws, wk, cj * 128, KC, lambda kc: hT[:, kc, t0:t0 + n], hkeys(bi), n)
                        k.op("act", lambda: nc.scalar.activation(out=vact[:, j, t0:t0 + n], in_=pt[:, 0:n],
                                                                 func=AF.Gelu_apprx_tanh,
                                                                 bias=vec("b_in", cfg.SV // 128 + j), scale=1.0),
                             r=[pk, ("vecs",)],
                             w=[("vact", j, bi)] + [xkey((3 * CC + j) // 2, b_) for b_ in range(3)])
                mod_blocks(2)
                if mod_in:
                    mod_blocks(3, nxt=True)
            k.fence()
            ln_fm(vact, "vact", CC, "sln_g", "sln_b", AF.Identity, R2, ABLK)

            dump("vact", vact, li=li)
            k.fence()
            R2.reset(KC * THALO * 2)
            wsT = R2.alloc(BF16, [CC, 128])
            bsb = R2.alloc(F32, [CC, 128])
            uT = [R2.alloc(BF16, [T]) for _ in range(2)]
            vn = [R2.alloc(BF16, [10, 128]) for _ in range(2)]
            RX.reset()
            sgt = [RX.alloc(F32, [512]) for _ in range(2)]
            wsTf = R2.alloc(F32, [CC, 128])
            k.dma("sp", wsTf, wsT_d[li * CC * 128:(li + 1) * CC * 128, :].rearrange("(g q) p -> q g p", q=128),
                  w=[("wsTf",)])
            k.op("act", lambda: nc.scalar.copy(out=wsT, in_=wsTf), r=[("wsTf",)], w=[("wsT",)])
            k.dma("sp", bsb.rearrange("p g n -> p (g n)"), bsb_d[li * 128:(li + 1) * 128, :], w=[("bsb",)])
            i2 = 0
            for j0 in range(0, CC, 2):
                ws, wk = w_next()
                for cj in range(2):
                    g = j0 + cj
                    ub = uT[g % 2]
                    vb = vn[g % 2]
                    for bi, (t0, n) in enumerate(ABLK):
                        pt, pk = proj(ws, wk, cj * 128, KC, lambda kc: hT[:, kc, t0:t0 + n], hkeys(bi), n)
                        k.op("act", lambda: nc.scalar.activation(out=ub[:, t0:t0 + n], in_=pt[:, 0:n],
                                                                 func=AF.Gelu_apprx_tanh,
                                                                 bias=vec("b_in", cfg.SU // 128 + g), scale=1.0),
                             r=[pk, ("vecs",)], w=[("uT", g % 2, bi)])
                    for bi, (t0, n) in enumerate(ABLK):
                        nt = n // 128
                        pt, pk = ps_get()
                        ptb = pt.bitcast(BF16)

                        def emit():
                            ins = None
                            for tt in range(nt):
                                ins = nc.tensor.transpose(ptb[:, tt * 128:(tt + 1) * 128],
                                                          vact[:, g, t0 + tt * 128:t0 + (tt + 1) * 128], ident)
                            return ins

                        k.op("pe", emit, r=[("vact", g, bi), ("ident",)], w=[pk])
                        k.op("dve", lambda: nc.vector.tensor_copy(
                            out=vb[:, t0 // 128:t0 // 128 + nt, :],
                            in_=ptb[:, 0:n].rearrange("p (a b) -> p a b", b=128)),
                            r=[pk], w=[("vn", g % 2, bi)])
                    for bi, (t0, n) in enumerate(ABLK):
                        nt = n // 128
                        pt, pk = ps_get()

                        def emit():
                            ins = None
                            for tt in range(nt):
                                ins = nc.tensor.matmul(pt[:, tt * 128:(tt + 1) * 128], lhsT=vb[:, t0 // 128 + tt, :],
                                                       rhs=wsT[:, g, :], start=True, stop=True)
                            return ins

                        k.op("pe", emit, r=[("vn", g % 2, bi), ("wsT",)], w=[pk])
                        sb = sgt[i2 % 2]
                        k.op("dve", lambda: nc.vector.tensor_tensor(
                            out=sb[:, 0:n].rearrange("p (a b) -> p a b", b=128),
                            in0=pt[:, 0:n].rearrange("p (a b) -> p a b", b=128),
                            in1=bsb[:, g:g + 1, :].broadcast_to([128, nt, 128]), op=ALU.add),
                            r=[pk, ("bsb",)], w=[("sgt", i2 % 2)])
                        k.op("dve", lambda: nc.vector.tensor_tensor(out=sguin[:, g, t0:t0 + n], in0=sb[:, 0:n],
                                                                    in1=ub[:, t0:t0 + n], op=ALU.mult),
                             r=[("sgt", i2 % 2), ("uT", g % 2, bi)], w=[("sguin", g, bi)])
                        i2 += 1
                mod_blocks(2)
                if mod_in:
                    mod_blocks(3, nxt=True)

            dump("sguin", sguin, li=li)
            k.fence()
            R2.reset(KC * THALO * 2)
            UBL = 15 + TOWN + 15
            UBC = 15 + TCTX + 15
            ubuf = [R2.alloc(BF16, [UBL + UBC]) for _ in range(2)]
            dg = R2.alloc(BF16, [CONV_WIDTH, 128])
            sgm = [R2.alloc(F32, [512]) for _ in range(2)]
            for u in ubuf:
                k.op("dve", lambda: nc.vector.memset(u, 0.0), w=[("ubuf", 0), ("ubuf", 1)])
            cblks = [(0, 512, 15), (512, 512, 15 + 512), (1024, 256, UBL + 15)]
            if last:
                cblks = cblks[:2]
            it = 0
            for j in range(CC):
                ws, wk = w_next()
                ub = ubuf[j % 2]
                ukey = ("ubuf", j % 2)
                for kk in range(CONV_WIDTH):
                    k.op("dve", lambda: nc.vector.tensor_scalar(out=dg[:, kk, :], in0=ident,
                                                                scalar1=vec("conv_w", j * 31 + kk), scalar2=None,
                                                                op0=ALU.mult),
                         r=[("ident",), ("vecs",)], w=[("dg",)])
                for (t0, n, uo), bi in zip(cblks, range(3)):
                    p1, k1 = proj(ws, wk, 0, KC, lambda kc: hT[:, kc, t0:t0 + n], hkeys(bi), n)
                    p2, k2 = proj(ws, wk, 128, KC, lambda kc: hT[:, kc, t0:t0 + n], hkeys(bi), n)
                    sb = sgm[it % 2]
                    k.op("act", lambda: nc.scalar.activation(out=sb[:, 0:n], in_=p2[:, 0:n], func=AF.Sigmoid,
                                                             bias=vec("b_in", cfg.A2 // 128 + j), scale=1.0),
                         r=[k2, ("vecs",)], w=[("sgm", it % 2)])
                    k.op("dve", lambda: nc.vector.scalar_tensor_tensor(
                        out=ub[:, uo:uo + n], in0=p1[:, 0:n], scalar=vec("b_in", cfg.A1 // 128 + j), in1=sb[:, 0:n],
                        op0=ALU.add, op1=ALU.mult), r=[k1, ("sgm", it % 2), ("vecs",)], w=[ukey])
                    it += 1
                p1, k1 = proj(ws, wk, 0, KC, lambda kc: h_halo[:, kc, 241:271], [("hh", kc) for kc in range(KC)], 30)
                p2, k2 = proj(ws, wk, 128, KC, lambda kc: h_halo[:, kc, 241:271], [("hh", kc) for kc in range(KC)], 30)
                sb = sgm[it % 2]
                k.op("act", lambda: nc.scalar.activation(out=sb[:, 0:30], in_=p2[:, 0:30], func=AF.Sigmoid,
                                                         bias=vec("b_in", cfg.A2 // 128 + j), scale=1.0),
                     r=[k2, ("vecs",)], w=[("sgm", it % 2)])
                k.op("dve", lambda: nc.vector.scalar_tensor_tensor(
                    out=sb[:, 0:30], in0=p1[:, 0:30], scalar=vec("b_in", cfg.A1 // 128 + j), in1=sb[:, 0:30],
                    op0=ALU.add, op1=ALU.mult), r=[k1, ("sgm", it % 2), ("vecs",)], w=[("sgm", it % 2)])
                k.op("dve", lambda: nc.vector.tensor_scalar(out=ub[:, 0:15], in0=sb[:, 0:15],
                                                            scalar1=halomask[:, 0:1], scalar2=None, op0=ALU.mult),
                     r=[("sgm", it % 2), ("halomask",)], w=[ukey])
                k.op("dve", lambda: nc.vector.tensor_scalar(out=ub[:, 15 + TOWN:UBL], in0=sb[:, 15:30],
                                                            scalar1=halomask[:, 1:2], scalar2=None, op0=ALU.mult),
                     r=[("sgm", it % 2), ("halomask",)], w=[ukey])
                it += 1
                for (o_in, t0, n, bi) in [(0, 0, 512, 0), (512, 512, 512, 1), (UBL, 1024, 256, 2)][:len(cblks)]:
                    pt, pk = ps_get()

                    def emit():
                        ins = None
                        for kk in range(CONV_WIDTH):
                            ins = nc.tensor.matmul(pt[:, 0:n], lhsT=dg[:, kk, :], rhs=ub[:, o_in + kk:o_in + kk + n],
                                                   start=(kk == 0), stop=(kk == CONV_WIDTH - 1))
                        return ins

                    k.op("pe", emit, r=[("dg",), ukey], w=[pk])
                    k.op("act", lambda: nc.scalar.activation(out=convin[:, j, t0:t0 + n], in_=pt[:, 0:n],
                                                             func=AF.Identity, bias=vec("conv_b", j), scale=1.0),
                         r=[pk, ("vecs",)], w=[("convin", j, bi)])
                mod_blocks(2)
                if mod_in:
                    mod_blocks(3, nxt=True)
            k.fence()
            R2.reset(KC * THALO * 2)
            ln_fm(convin, "convin", CC, "cln_g", "cln_b", AF.Silu, R2, ABLK)

            dump("convin", convin, li=li)
            k.fence()
            R2.reset(KC * THALO * 2)
            RX.reset()
            ps_pools([0, 1, 2, 3], [4, 5, 6, 7])
            tabb = R2.alloc(BF16, [2, 16, 64])
            QT = R2.alloc(BF16, [20, 128])
            KT = R2.alloc(BF16, [TALL])
            VT = R2.alloc(BF16, [TALL])
            Vt = R2.alloc(BF16, [14, 128])
            ET = [RX.alloc(BF16, [8, 128]) for _ in range(2)]
            tab = RX.alloc(F32, [16, 64])
            rs = RX.alloc(F32, [4, 128])
            k.op("dve", lambda: nc.vector.memset(QT, 0.0), w=[("QT",)])
            allblk = [(0, 512, hT, "h", 0), (512, 512, hT, "h", 1), (1024, 256, hT, "h", 2), (1280, 512, h_halo, "hh", None)]
            ie = 0
            nqrows = 16 if last else 20
            for hp in range(HP):
                ws, wk = w_next()
                for hh_ in range(2):
                    k.dma("sp", tab.rearrange("p b c -> p (b c)"),
                          tab_d[li * 128:(li + 1) * 128, (2 * hp + hh_) * 1024:(2 * hp + hh_ + 1) * 1024],
                          w=[("tab",)])
                    k.op("act", lambda: nc.scalar.copy(out=tabb[:, hh_, :, :], in_=tab), r=[("tab",)],
                         w=[("tabb",)])
                for bi, (t0, n) in enumerate(BLKS):
                    if last and bi == 2:
                        continue
                    pt, pk = proj(ws, wk, 0, KC, lambda kc: hT[:, kc, t0:t0 + n], hkeys(bi), n)
                    nr = n // 64
                    for hh in range(2):
                        pr = slice(hh * 64, (hh + 1) * 64)
                        k.op("dve", lambda: nc.vector.tensor_scalar(
                            out=QT[pr, t0 // 64:t0 // 64 + nr, hh * 64:(hh + 1) * 64],
                            in0=pt[pr, 0:n].rearrange("p (r c) -> p r c", c=64),
                            scalar1=vec("b_in", cfg.Q // 128 + hp)[pr, :], scalar2=0.125, op0=ALU.add, op1=ALU.mult),
                            r=[pk, ("vecs",)], w=[("QT",)])
                for (t0, n, src, skey, bi) in allblk:
                    rk = hkeys(bi) if bi is not None else [("hh", kc) for kc in range(KC)]
                    sfn = (lambda kc: hT[:, kc, t0:t0 + n]) if bi is not None else (lambda kc: h_halo[:, kc, :])
                    pt, pk = proj(ws, wk, 128, KC, sfn, rk, n)
                    k.op("act", lambda: nc.scalar.activation(out=KT[:, t0:t0 + n], in_=pt[:, 0:n], func=AF.Identity,
                                                             bias=vec("b_in", cfg.K // 128 + hp), scale=1.0),
                         r=[pk, ("vecs",)], w=[("KT",)])
                ws, wk = w_next()
                for (t0, n, src, skey, bi) in allblk:
                    rk = hkeys(bi) if bi is not None else [("hh", kc) for kc in range(KC)]
                    sfn = (lambda kc: hT[:, kc, t0:t0 + n]) if bi is not None else (lambda kc: h_halo[:, kc, :])
                    pt, pk = proj(ws, wk, 0, KC, sfn, rk, n)
                    k.op("act", lambda: nc.scalar.activation(out=VT[:, t0:t0 + n], in_=pt[:, 0:n], func=AF.Identity,
                                                             bias=vec("b_in", cfg.V // 128 + hp), scale=1.0),
                         r=[pk, ("vecs",)], w=[("VT",)])
                for t4 in range(0, 14, 4):
                    nt = min(4, 14 - t4)
                    pt, pk = ps_get()
                    ptb = pt.bitcast(BF16)

                    def emit():
                        ins = None
                        for tt in range(nt):
                            ins = nc.tensor.transpose(ptb[:, tt * 128:(tt + 1) * 128],
                                                      VT[:, (t4 + tt) * 128:(t4 + tt + 1) * 128], ident)
                        return ins

                    k.op("pe", emit, r=[("VT",), ("ident",)], w=[pk])
                    k.op("act", lambda: nc.scalar.copy(out=Vt[:, t4:t4 + nt, :],
                                                       in_=ptb[:, 0:nt * 128].rearrange("p (a b) -> p a b", b=128)),
                         r=[pk], w=[("Vt",)])
                def att_row(j, rr, po, ko, pS, kS, iee):
                    if j < 16:
                        loc = chunk_list(j)
                        chunks = [(pair_tok(P), True) for P in loc] + [(1024, False), (1152, False)]
                    else:
                        loc = []
                        chunks = [(1024, False), (1152, False)]
                    nloc = len(loc)
                    nch = len(chunks)
                    st = {}

                    def sreg(ci):
                        return (st["psa"] if ci < 4 else st["psb"])[:, (ci % 4) * 128:(ci % 4 + 1) * 128]

                    eb = ET[iee % 2]
                    ekey = ("ET", iee % 2)

                    def stage_s():
                        st["psa"], st["ka"] = ps_get()
                        st["psb"], st["kb"] = ps_get()

                        def emit():
                            ins = None
                            for ci, (tk, isloc) in enumerate(chunks):
                                ins = nc.tensor.matmul(sreg(ci), lhsT=KT[:, tk:tk + 128], rhs=QT[:, j, :],
                                                       start=True, stop=not isloc)
                                if isloc:
                                    s_ = 2 * loc[ci] - j + 4
                                    nc.tensor.matmul(sreg(ci), lhsT=ident, rhs=tabb[:, :, s_, :],
                                                     start=False, stop=False)
                                    idx = j * 6 + ci
                                    ins = nc.tensor.matmul(sreg(ci), lhsT=selb[0:2, :],
                                                           rhs=rm2b[0:2, idx:idx + 1].broadcast_to([2, 128]),
                                                           start=False, stop=True)
                            return ins

                        k.op("pe", emit, r=[("KT",), ("QT",), ("tabb",), ("ident",), ("selb",), ("rm2b",)],
                             w=[st["ka"], st["kb"]])

                    def stage_e():
                        ka, kb = st["ka"], st["kb"]
                        na = min(nch, 4)
                        k.op("act", lambda: nc.scalar.activation(
                            out=eb[:, 0:na, :], in_=st["psa"][:, 0:na * 128].rearrange("p (a b) -> p a b", b=128),
                            func=AF.Exp), r=[ka], w=[ekey])
                        if nch > 4:
                            k.op("act", lambda: nc.scalar.activation(
                                out=eb[:, 4:nch, :],
                                in_=st["psb"][:, 0:(nch - 4) * 128].rearrange("p (a b) -> p a b", b=128),
                                func=AF.Exp), r=[kb], w=[ekey])

                    def stage_pv():
                        def emit2():
                            ins = None
                            for ci, (tk, _) in enumerate(chunks):
                                nc.tensor.matmul(po[:, rr * 128:(rr + 1) * 128], lhsT=Vt[:, tk // 128, :],
                                                 rhs=eb[:, ci, :], start=(ci == 0), stop=(ci == nch - 1))
                                ins = nc.tensor.matmul(pS[:, rr * 128:(rr + 1) * 128], lhsT=ones1,
                                                       rhs=eb[:, ci, :], start=(ci == 0), stop=(ci == nch - 1))
                            return ins

                        k.op("pe", emit2, r=[ekey, ("Vt",), ("ones1",)], w=[ko, kS])
                        if rr == 3:
                            r4 = j - 3
                            k.op("dve", lambda: nc.vector.reciprocal(out=rs.rearrange("p a b -> p (a b)"),
                                                                     in_=pS[:, :]), r=[kS], w=[("rs",)])
                            tq = r4 * 64
                            bi_q = 0 if tq < 512 else (1 if tq < 1024 else 2)
                            for hh in range(2):
                                pr = slice(hh * 64, (hh + 1) * 64)
                                cs = slice(hh * 64, (hh + 1) * 64)
                                k.op("dve", lambda: nc.vector.tensor_tensor(
                                    out=att[pr, hp, tq:tq + 256].rearrange("p (r c) -> p r c", c=64),
                                    in0=po[pr, :].rearrange("p (r c) -> p r c", c=128)[:, :, cs],
                                    in1=rs[pr, :, cs], op=ALU.mult),
                                    r=[ko, ("rs",)], w=[("att", hp, bi_q)])

                    return stage_s, stage_e, stage_pv

                pending = None
                grp = None
                for j in range(nqrows):
                    rr = j % 4
                    if rr == 0:
                        grp = ps_get("acc") + ps_get("acc")
                    st_s, st_e, st_pv = att_row(j, rr, grp[0], grp[1], grp[2], grp[3], ie)
                    ie += 1
                    st_s()
                    if pending is not None:
                        pending()
                    st_e()
                    pending = st_pv
                pending()

            dump("att", att, li=li)
            k.fence()
            ps_pools(list(range(8)), [])
            R2.reset()
            RX.reset()
            bog_derive(0)
            mod_derive(1)
            bog_derive(1)
            merged = R2.alloc(BF16, [KC, T])
            sgb = [RX.alloc(F32, [512]) for _ in range(2)]
            tbb = [RX.alloc(F32, [512]) for _ in range(2)]
            accb = RX.alloc(F32, [T])
            brin = [(convin, "convin"), (sguin, "sguin"), (att, "att")]
            nblk_b = 2 if last else 3
            BBLK = BLKS[:2] if last else (BLKS[:2] + [(1024, TQ)])

            def gather_q(buf3, nch, rkeys, wkeys):
                dst = buf3[:, 0:nch, 1024:1024 + TQ]
                k.op("dve", lambda: nc.vector.tensor_scalar(out=dst, in0=dst, scalar1=qmask[:, 0:1], scalar2=None,
                                                            op0=ALU.mult), r=rkeys + [("qmask",)], w=wkeys)
                for i in range(1, 4):
                    k.op("dve", lambda: nc.vector.scalar_tensor_tensor(
                        out=dst, in0=buf3[:, 0:nch, 1024 + i * TQ:1024 + (i + 1) * TQ], scalar=qmask[:, i:i + 1],
                        in1=dst, op0=ALU.mult, op1=ALU.add), r=rkeys + [("qmask",)], w=wkeys)

            if not last:
                gather_q(hT, KC, [("h", kc, 2) for kc in range(KC)], [("h", kc, 2) for kc in range(KC)])
                for buf, bkey in [(convin, "convin"), (sguin, "sguin"), (att, "att")]:
                    gather_q(buf, CC, [(bkey, c, 2) for c in range(CC)], [(bkey, c, 2) for c in range(CC)])
            ib = 0
            for f in range(KC):
                if f > 0 and not mod_in:
                    mod_blocks(1, nxt=True)
                for b in range(3):
                    ws, wk = w_next()
                    buf, bkey = brin[b]
                    for bi, (t0, n) in enumerate(BBLK):
                        pg, kg = proj(ws, wk, 0, KC, lambda kc: hT[:, kc, t0:t0 + n], hkeys(bi), n)
                        py, ky = proj(ws, wk, 128, CC, lambda kc: buf[:, kc, t0:t0 + n],
                                      [(bkey, c, bi) for c in range(CC)], n)
                        sb = sgb[ib % 2]
                        k.op("act", lambda: nc.scalar.activation(out=sb[:, 0:n], in_=pg[:, 0:n], func=AF.Sigmoid,
                                                                 bias=vec("b_in", cfg.G // 128 + b * KC + f),
                                                                 scale=1.0),
                             r=[kg, ("vecs",)], w=[("sgb", ib % 2)])
                        if b == 0:
                            k.op("dve", lambda: nc.vector.tensor_tensor(out=accb[:, t0:t0 + n], in0=py[:, 0:n],
                                                                        in1=sb[:, 0:n], op=ALU.mult),
                                 r=[ky, ("sgb", ib % 2)], w=[("accb", bi)])
                        else:
                            tb = tbb[ib % 2]
                            k.op("dve", lambda: nc.vector.tensor_tensor(out=tb[:, 0:n], in0=py[:, 0:n],
                                                                        in1=sb[:, 0:n], op=ALU.mult),
                                 r=[ky, ("sgb", ib % 2)], w=[("tbb", ib % 2)])
                            dst = accb[:, t0:t0 + n] if b == 1 else merged[:, f, t0:t0 + n]
                            wkeys = [("accb", bi)] if b == 1 else [("merged", f, bi)]
                            k.op("dve", lambda: nc.vector.tensor_tensor(out=dst, in0=tb[:, 0:n],
                                                                        in1=accb[:, t0:t0 + n], op=ALU.add),
                                 r=[("tbb", ib % 2), ("accb", bi)], w=wkeys)
                        ib += 1

            if not mod_in:
                mod_blocks(1, nxt=True)
            dump("merged", merged, li=li)
            k.fence()
            for kc in range(KC):
                k.dma("sp", xT[:, kc, :], xsp_d[kc * 128:(kc + 1) * 128, :], r=[("xsp", kc)],
                      w=[xkey(kc, bi) for bi in range(3)])
            if not last:
                gather_q(xT, KC, [xkey(kc, 2) for kc in range(KC)], [xkey(kc, 2) for kc in range(KC)])
            for o0 in range(0, KC, 2):
                ws, wk = w_next()
                for oj in range(2):
                    o = o0 + oj
                    for bi, (t0, n) in enumerate(BBLK):
                        r = 0 if bi < 2 else 1
                        pt, pk = proj(ws, wk, oj * 128, KC, lambda kc: merged[:, kc, t0:t0 + n],
                                      [("merged", c, bi) for c in range(KC)], n)
                        k.op("dve", lambda: nc.vector.scalar_tensor_tensor(
                            out=xT[:, o, t0:t0 + n], in0=pt[:, 0:n], scalar=mod(2, o, r), in1=xT[:, o, t0:t0 + n],
                            op0=ALU.mult, op1=ALU.add), r=[pk, xkey(o, bi), ("mod", 2)], w=[xkey(o, bi)])
                        k.op("act", lambda: nc.scalar.activation(out=xT[:, o, t0:t0 + n], in_=xT[:, o, t0:t0 + n],
                                                                 func=AF.Identity, bias=bog[:, 0, o, r:r + 1],
                                                                 scale=1.0),
                             r=[xkey(o, bi), ("bog", 0)], w=[xkey(o, bi)])

            dump("xmid", xT, li=li)
            R2.reset()
            RX.reset()
            fT = R2.alloc(BF16, [KC, T])
            scratch = ([RX.alloc(BF16, [512]) for _ in range(2)], RX.alloc(F32, [512]),
                       [RX.alloc(F32, [512]) for _ in range(2)])
            for bi, (t0, n) in enumerate(BBLK):
                r = 0 if bi < 2 else 1
                rms_mod(lambda kc: xT[:, kc, t0:t0 + n], lambda kc: [xkey(kc, bi)], n,
                        lambda kc: hT[:, kc, t0:t0 + n], lambda kc: [("h", kc, bi)],
                        lambda kc: Amod[:, 1, kc, r:r + 1], lambda kc: mod(3, kc, r), scratch, [("mod", 3), ("amod", 1)])
            rl = scratch[2]
            ir = 0
            dcnt_c = [0]
            for g in range(FC // KC):
                for j0 in range(0, KC, 2):
                    ws, wk = w_next()
                    for cj in range(2):
                        j = j0 + cj
                        for bi, (t0, n) in enumerate(BBLK):
                            pt, pk = proj(ws, wk, cj * 128, KC, lambda kc: hT[:, kc, t0:t0 + n], hkeys(bi), n)
                            rb = rl[ir % 2]
                            k.op("act", lambda: nc.scalar.activation(out=rb[:, 0:n], in_=pt[:, 0:n], func=AF.Relu,
                                                                     bias=vec("b_ff1", g * KC + j), scale=1.0),
                                 r=[pk, ("vecs",)], w=[("ntmp", ir % 2)])
                            k.op("dve", lambda: nc.vector.tensor_tensor(out=fT[:, j, t0:t0 + n], in0=rb[:, 0:n],
                                                                        in1=rb[:, 0:n], op=ALU.mult),
                                 r=[("ntmp", ir % 2)], w=[("merged", j, bi)])
                            ir += 1
                    dcnt_c[0] += 1
                    if not mod_in and dcnt_c[0] % 2 == 0:
                        mod_blocks(1, nxt=True)
                for o0 in range(0, KC, 2):
                    ws, wk = w_next()
                    for oj in range(2):
                        o = o0 + oj
                        for bi, (t0, n) in enumerate(BBLK):
                            r = 0 if bi < 2 else 1
                            pt, pk = proj(ws, wk, oj * 128, KC, lambda kc: fT[:, kc, t0:t0 + n],
                                          [("merged", c, bi) for c in range(KC)], n)
                            k.op("dve", lambda: nc.vector.scalar_tensor_tensor(
                                out=xT[:, o, t0:t0 + n], in0=pt[:, 0:n], scalar=mod(5, o, r), in1=xT[:, o, t0:t0 + n],
                                op0=ALU.mult, op1=ALU.add), r=[pk, xkey(o, bi), ("mod", 5)], w=[xkey(o, bi)])
                            if g == FC // KC - 1:
                                k.op("act", lambda: nc.scalar.activation(
                                    out=xT[:, o, t0:t0 + n], in_=xT[:, o, t0:t0 + n], func=AF.Identity,
                                    bias=bog[:, 1, o, r:r + 1], scale=1.0),
                                    r=[xkey(o, bi), ("bog", 1)], w=[xkey(o, bi)])
                        if g == FC // KC - 1 and not final_norm and NL == 1:
                            k.dma("sp", out_d[o * 128:(o + 1) * 128, :], xT[:, o, :],
                                  r=[xkey(o, bi) for bi in range(3)], w=[("out", o)])
                    dcnt_c[0] += 1
                    if not mod_in and dcnt_c[0] % 2 == 0:
                        mod_blocks(1, nxt=True)

        k.fence()
        R2.reset()
        if final_norm:
            sq = [R2.alloc(BF16, [512]) for _ in range(2)]
            rstd = R2.alloc(F32, [512])
            obuf = [R2.alloc(F32, [512]) for _ in range(2)]
            io = 0
            for bi, (t0, n) in enumerate(BLKS[:2]):
                pt, pk = ps_get()
                for kc in range(KC):
                    sqb = sq[kc % 2]
                    k.op("act", lambda: nc.scalar.activation(out=sqb[:, 0:n], in_=xT[:, kc, t0:t0 + n], func=AF.Square),
                         r=[xkey(kc, bi)], w=[("sq", kc % 2)])
                    k.op("pe", lambda: nc.tensor.matmul(pt[:, 0:n], lhsT=onesD, rhs=sqb[:, 0:n],
                                                        start=(kc == 0), stop=(kc == KC - 1)),
                         r=[("sq", kc % 2), ("onesD",)], w=[pk])
                k.op("act", lambda: nc.scalar.activation(out=rstd[:, 0:n], in_=pt[:, 0:n], func=AF.Sqrt, bias=EPS,
                                                         scale=1.0), r=[pk], w=[("rstd",)])
                k.op("dve", lambda: nc.vector.reciprocal(out=rstd[:, 0:n], in_=rstd[:, 0:n]), r=[("rstd",)],
                     w=[("rstd",)])
                for kc in range(KC):
                    ob = obuf[io % 2]
                    k.op("dve", lambda: nc.vector.scalar_tensor_tensor(out=ob[:, 0:n], in0=xT[:, kc, t0:t0 + n],
                                                                       scalar=fgv[:, kc:kc + 1], in1=rstd[:, 0:n],
                                                                       op0=ALU.mult, op1=ALU.mult),
                         r=[xkey(kc, bi), ("rstd",), ("fgv",)], w=[("obuf", io % 2)])
                    k.dma("sp", out_d[kc * 128:(kc + 1) * 128, t0:t0 + n], ob[:, 0:n], r=[("obuf", io % 2)],
                          w=[("out", kc, bi)])
                    io += 1
        elif NL != 1:
            for kc in range(KC):
                k.dma("sp", out_d[kc * 128:(kc + 1) * 128, :], xT[:, kc, :], r=[xkey(kc, bi) for bi in range(3)],
                      w=[("out", kc)])
        if mod_next:
            k.dma("sp", modout_d[:, :], modvn.rearrange("p a b -> p (a b)"), r=[("modn",)], w=[("modout",)])
        sp = k.eng["sp"]
        for i, s in enumerate(k.dsems):
            if k.dsem_val[i] > 0:
                sp.h.wait_ge(s, k.dsem_val[i])
        assert wq_used[0] == len(wq), (wq_used[0], len(wq))
    return nc


def _prep_common(cfg, inp, layers):
    L = len(layers)
    D = cfg.D
    f = lambda a: np.ascontiguousarray(np.asarray(a, np.float32))
    com = {
        "vecs": np.concatenate([pack_vecs(cfg, inp, l) for l in layers], 0),
        "final_g": fm(inp["final_g"]),
        "tab": np.concatenate([build_tab(cfg, inp["na_rpb"][l]) for l in layers], 0),
        "wsT": np.concatenate([f(np.asarray(inp["sgu_w"][l]).transpose(0, 2, 1)).reshape(cfg.CC * 128, 128)
                               for l in layers], 0),
        "bsb": np.concatenate([np.broadcast_to(f(inp["sgu_b"][l]).reshape(1, cfg.CC * 128), (128, cfg.CC * 128))
                               for l in layers], 0).copy(),
    }
    for name, key in [("w_mod", "w_mod"), ("w_in", "w_in"), ("w_conv_out", "w_conv_out"), ("w_sgu_out", "w_sgu_out"),
                      ("w_na_out", "w_na_out"), ("w_o", "w_o"), ("w_ff1", "w_ff1"), ("w_ff2", "w_ff2")]:
        a = np.asarray(inp[key], np.float32)
        sel = a[layers[0]:layers[-1] + 1] if list(layers) == list(range(layers[0], layers[-1] + 1)) else a[list(layers)]
        com[name] = np.ascontiguousarray(sel).reshape(L * a.shape[1], a.shape[2])
    return com


def _core_static(cfg, inp, core):
    b, q = core // 4, core % 4
    cT = np.stack([fm(inp["c"][b]), fm(inp["c_ctx"])], axis=-1).reshape(128, cfg.KC * 2)
    hm = np.zeros((128, 2), np.float32)
    hm[:, 0] = 1.0 if q > 0 else 0.0
    hm[:, 1] = 1.0 if q < 3 else 0.0
    return {"cT": np.ascontiguousarray(cT), "halomask": hm,
            "ident": np.eye(128, dtype=np.float32),
            "qmask": np.ascontiguousarray(np.broadcast_to(np.eye(4, dtype=np.float32)[q][None, :], (128, 4))),
            "rm2": np.ascontiguousarray(build_rowmask(q).reshape(2, 64, 96)[:, 0, :]),
            "sel": np.ascontiguousarray(np.repeat(np.eye(2, dtype=np.float32), 64, axis=1))}


def _halo_from(xl_full, core):
    b, q = core // 4, core % 4
    D = xl_full.shape[-1]
    h = np.zeros((THALO, D), np.float32)
    t0 = q * TOWN
    if q > 0:
        h[0:256] = xl_full[b, t0 - 256:t0]
    if q < 3:
        h[256:512] = xl_full[b, t0 + TOWN:t0 + TOWN + 256]
    return np.ascontiguousarray(h.T)


_PROG_CACHE = {}


def run_unfused(cfg, inp, trace=None, debug=False, dbg_out=None):
    L = cfg.L
    xl = np.asarray(inp["x"], np.float32)
    xc = np.asarray(inp["ctx"], np.float32)
    statics = [_core_static(cfg, inp, c) for c in range(8)]
    out = None
    for l in range(L):
        last = l == L - 1
        mod_in = l > 0
        mod_next = not last
        keyp = (cfg.D, last, mod_in, mod_next)
        if keyp not in _PROG_CACHE:
            _PROG_CACHE[keyp] = build_program(cfg, [0], final_norm=last, debug=(debug and l == 0),
                                              mod_in=mod_in, mod_next=mod_next)
        nc = _PROG_CACHE[keyp]
        com = _prep_common(cfg, inp, [l])
        if mod_in:
            del com["w_mod"]
        if mod_next:
            com["w_mod_next"] = np.ascontiguousarray(np.asarray(inp["w_mod"][l + 1], np.float32))
        in_maps = []
        for c in range(8):
            b, q = c // 4, c % 4
            xo = np.concatenate([xl[b, q * TOWN:(q + 1) * TOWN], xc[b]], 0)
            m = dict(com)
            m.update(statics[c])
            m["xT"] = np.ascontiguousarray(xo.T)
            m["xhT"] = _halo_from(xl, c)
            if mod_in:
                m["modv_in"] = modv_prev[c]
            in_maps.append(m)
        res = run_bass_kernel_spmd(nc, in_maps, core_ids=list(range(8)))
        if mod_next:
            modv_prev = [np.ascontiguousarray(res.results[c]["modv_out"]) for c in range(8)]
        if dbg_out is not None and l == 0:
            for c in range(8):
                dbg_out.append({kk: vv for kk, vv in res.results[c].items() if kk.startswith('dbg_')})
        if last:
            out = np.zeros((BATCH, SEQ, cfg.D), np.float32)
            for c in range(8):
                b, q = c // 4, c % 4
                out[b, q * TOWN:(q + 1) * TOWN] = res.results[c]["outT"].T
        else:
            xl_n = np.zeros_like(xl)
            xc_n = np.zeros_like(xc)
            for c in range(8):
                b, q = c // 4, c % 4
                o = res.results[c]["outT"].T
                xl_n[b, q * TOWN:(q + 1) * TOWN] = o[:TOWN]
                xc_n[b, q * TQ:(q + 1) * TQ] = o[TOWN:TOWN + TQ]
            xl, xc = xl_n, xc_n
            if trace is not None:
                trace.append((xl, xc))
    return out


def kernel(**inputs):
    cfg = Cfg(2048, 4)
    return run_unfused(cfg, inputs)
```
